# Optimizing a Trainium2 kernel written in Bass

```python
import jax, jax.numpy as jnp
from jax import lax
import numpy as np

D_MODEL = 1024
BATCH = 8
SEQ = 4096
DEPTH = 4

GRID_W = 64
HEAD_DIM = 64
A_HEADS = 8
A_KV_HEADS = 2
B_HEADS = 8
B_KV_HEADS = 2
C_HEADS = D_MODEL // HEAD_DIM
Q_BLOCK = 128
WINDOW = 128
NA_KH = 8
NA_KW = 16
MEM_TOKENS = 256
MEM_HEADS = 4
MEM_HEAD_DIM = D_MODEL // MEM_HEADS
D_FF = 256 * ((8 * D_MODEL // 3 + 255) // 256)
ROPE_THETA = 10000.0
LN_EPS = 1e-5
RMS_EPS = 1e-6
ALPHA = (2.0 * DEPTH) ** 0.25
BETA = (8.0 * DEPTH) ** -0.25
N_EVEN = (DEPTH + 1) // 2
N_ODD = DEPTH // 2

A_Q = A_HEADS * HEAD_DIM
A_KV = A_KV_HEADS * HEAD_DIM
B_Q = B_HEADS * HEAD_DIM
B_KV = B_KV_HEADS * HEAD_DIM
AB_IN = A_Q + 2 * A_KV + B_Q + 2 * B_KV
AB_OUT = (A_HEADS + B_HEADS) * HEAD_DIM
AB_SPLITS = [A_Q, A_Q + A_KV, A_Q + 2 * A_KV, A_Q + 2 * A_KV + B_Q, A_Q + 2 * A_KV + B_Q + B_KV]
C_WIDTH = C_HEADS * HEAD_DIM

kernel_name = "hybrid_axial_window_neighbourhood_encoder"


def layer_norm(x, g, b):
    xf = x.astype(jnp.float32)
    mu = xf.mean(-1, keepdims=True)
    var = jnp.square(xf - mu).mean(-1, keepdims=True)
    return ((xf - mu) * lax.rsqrt(var + LN_EPS) * g.astype(jnp.float32) + b.astype(jnp.float32)).astype(x.dtype)


def rms_norm(x, g):
    xf = x.astype(jnp.float32)
    return (xf * lax.rsqrt(jnp.mean(xf * xf, -1, keepdims=True) + RMS_EPS) * g.astype(jnp.float32)).astype(x.dtype)


def swiglu(x, w_gate, w_up, w_down):
    return (jax.nn.silu(x @ w_gate) * (x @ w_up)) @ w_down


def rope_angles(pos, dim):
    inv = ROPE_THETA ** (-jnp.arange(0, dim, 2, dtype=jnp.float32) / dim)
    return pos[:, None] * inv[None, :]


def apply_rope(x, ang):
    half = x.shape[-1] // 2
    cos = jnp.cos(ang)[None, :, None, :].astype(x.dtype)
    sin = jnp.sin(ang)[None, :, None, :].astype(x.dtype)
    x1, x2 = x[..., :half], x[..., half:]
    return jnp.concatenate([x1 * cos - x2 * sin, x2 * cos + x1 * sin], axis=-1)


def global_gqa(q, k, v):
    B, S, H, d = q.shape
    Hkv = k.shape[2]
    G = H // Hkv
    nb = S // Q_BLOCK
    scale = d ** -0.5
    qb = q.reshape(B, nb, Q_BLOCK, Hkv, G, d).transpose(1, 0, 2, 3, 4, 5)

    def one_block(q_blk):
        s = jnp.einsum('bqkgd,bskd->bkgqs', q_blk, k).astype(jnp.float32) * scale
        p = jax.nn.softmax(s, axis=-1).astype(v.dtype)
        return jnp.einsum('bkgqs,bskd->bqkgd', p, v)

    o = lax.map(one_block, qb)
    return o.transpose(1, 0, 2, 3, 4, 5).reshape(B, S, H * d)


def window_gqa_sink(q, k, v, sink):
    B, S, H, d = q.shape
    Hkv = k.shape[2]
    G = H // Hkv
    nb = S // Q_BLOCK
    scale = d ** -0.5
    qb = q.reshape(B, nb, Q_BLOCK, Hkv, G, d)
    pad = ((0, 0), (Q_BLOCK, Q_BLOCK), (0, 0), (0, 0))
    kp = jnp.pad(k, pad).reshape(B, nb + 2, Q_BLOCK, Hkv, d)
    vp = jnp.pad(v, pad).reshape(B, nb + 2, Q_BLOCK, Hkv, d)
    k_band = jnp.concatenate([kp[:, :-2], kp[:, 1:-1], kp[:, 2:]], axis=2)
    v_band = jnp.concatenate([vp[:, :-2], vp[:, 1:-1], vp[:, 2:]], axis=2)
    blk = jnp.arange(nb)[:, None] * Q_BLOCK
    qi = blk + jnp.arange(Q_BLOCK)[None, :]
    kj = blk - Q_BLOCK + jnp.arange(3 * Q_BLOCK)[None, :]
    rel = kj[:, None, :] - qi[:, :, None]
    valid = (jnp.abs(rel) <= WINDOW) & (kj[:, None, :] >= 0) & (kj[:, None, :] < S)
    s = jnp.einsum('bnqkgd,bnskd->bnkgqs', qb, k_band).astype(jnp.float32) * scale
    s = jnp.where(valid[None, :, None, None], s, -jnp.inf)
    sink_l = sink.astype(jnp.float32).reshape(Hkv, G)[None, None, :, :, None, None]
    m = jnp.maximum(s.max(-1, keepdims=True), sink_l)
    p = jnp.exp(s - m)
    p = (p / (p.sum(-1, keepdims=True) + jnp.exp(sink_l - m))).astype(v.dtype)
    o = jnp.einsum('bnkgqs,bnskd->bnqkgd', p, v_band)
    return o.reshape(B, S, H * d)


def mixer_ab(h, w_in, w_out, q_gain, k_gain, sink, ang_2d, ang_1d):
    B, S, _ = h.shape
    qa, ka, va, qb, kb, vb = jnp.split(h @ w_in, AB_SPLITS, axis=-1)
    qa = qa.reshape(B, S, A_HEADS, HEAD_DIM)
    ka = ka.reshape(B, S, A_KV_HEADS, HEAD_DIM)
    va = va.reshape(B, S, A_KV_HEADS, HEAD_DIM)
    qa = apply_rope(rms_norm(qa, q_gain), ang_2d)
    ka = apply_rope(rms_norm(ka, k_gain), ang_2d)
    out_a = global_gqa(qa, ka, va)
    qb = apply_rope(qb.reshape(B, S, B_HEADS, HEAD_DIM), ang_1d)
    kb = apply_rope(kb.reshape(B, S, B_KV_HEADS, HEAD_DIM), ang_1d)
    vb = vb.reshape(B, S, B_KV_HEADS, HEAD_DIM)
    out_b = window_gqa_sink(qb, kb, vb, sink)
    return jnp.concatenate([out_a, out_b], axis=-1) @ w_out


def mixer_c(h, w_in, w_out, rpb):
    B, S, _ = h.shape
    rows = S // GRID_W
    kh = min(NA_KH, rows)
    kw = NA_KW
    scale = HEAD_DIM ** -0.5
    q, k, v = jnp.split(h @ w_in, 3, axis=-1)
    qg = q.reshape(B, rows, GRID_W, C_HEADS, HEAD_DIM)
    kg = k.reshape(B, rows, GRID_W, C_HEADS, HEAD_DIM)
    vg = v.reshape(B, rows, GRID_W, C_HEADS, HEAD_DIM)
    col = jnp.arange(GRID_W)
    col_start = jnp.clip(col - kw // 2, 0, GRID_W - kw)
    col_idx = col_start[:, None] + jnp.arange(kw)[None, :]
    dc = col_idx - col[:, None]

    def row_block(r):
        rs = jnp.clip(r - kh // 2, 0, rows - kh)
        k_rows = lax.dynamic_slice_in_dim(kg, rs, kh, axis=1)
        v_rows = lax.dynamic_slice_in_dim(vg, rs, kh, axis=1)
        k_nb = jnp.take(k_rows, col_idx, axis=2)
        v_nb = jnp.take(v_rows, col_idx, axis=2)
        q_r = lax.dynamic_index_in_dim(qg, r, axis=1, keepdims=False)
        dr = rs + jnp.arange(kh) - r
        bias = rpb[:, dr[None, :, None] + NA_KH - 1, dc[:, None, :] + NA_KW - 1]
        s = jnp.einsum('bwhd,bawjhd->bhwaj', q_r, k_nb).astype(jnp.float32) * scale
        s = s + bias.astype(jnp.float32)[None]
        p = jax.nn.softmax(s.reshape(B, C_HEADS, GRID_W, kh * kw), axis=-1)
        p = p.reshape(B, C_HEADS, GRID_W, kh, kw).astype(v.dtype)
        return jnp.einsum('bhwaj,bawjhd->bwhd', p, v_nb)

    o = lax.map(row_block, jnp.arange(rows))
    o = o.transpose(1, 0, 2, 3, 4).reshape(B, S, C_WIDTH)
    return o @ w_out


def memory_attn(h, mem, w_q, w_kv, w_o):
    B, S, _ = h.shape
    M = mem.shape[1]
    q = (h @ w_q).reshape(B, S, MEM_HEADS, MEM_HEAD_DIM)
    k, v = jnp.split(mem @ w_kv, 2, axis=-1)
    k = k.reshape(B, M, MEM_HEADS, MEM_HEAD_DIM)
    v = v.reshape(B, M, MEM_HEADS, MEM_HEAD_DIM)
    s = jnp.einsum('bshd,bmhd->bhsm', q, k).astype(jnp.float32) * (MEM_HEAD_DIM ** -0.5)
    p = jax.nn.softmax(s, axis=-1).astype(v.dtype)
    o = jnp.einsum('bhsm,bmhd->bshd', p, v).reshape(B, S, D_MODEL)
    return o @ w_o


def setup_inputs(seed: int = 0) -> dict:
    key = jax.random.key(seed)
    ks = jax.random.split(key, 20)
    nrm = jax.random.normal
    f32 = jnp.float32
    d_sc = D_MODEL ** -0.5
    return {
        "x": nrm(ks[0], (BATCH, SEQ, D_MODEL), f32),
        "mem": nrm(ks[1], (BATCH, MEM_TOKENS, D_MODEL), f32),
        "ln_g": 1.0 + 0.02 * nrm(ks[2], (DEPTH, 4, D_MODEL), f32),
        "ln_b": 0.02 * nrm(ks[3], (DEPTH, 4, D_MODEL), f32),
        "ffn_w_gate": nrm(ks[4], (DEPTH, 2, D_MODEL, D_FF), f32) * d_sc,
        "ffn_w_up": nrm(ks[5], (DEPTH, 2, D_MODEL, D_FF), f32) * d_sc,
        "ffn_w_down": nrm(ks[6], (DEPTH, 2, D_FF, D_MODEL), f32) * (D_FF ** -0.5) * BETA,
        "ab_w_in": nrm(ks[7], (N_EVEN, D_MODEL, AB_IN), f32) * d_sc,
        "ab_w_out": nrm(ks[8], (N_EVEN, AB_OUT, D_MODEL), f32) * (AB_OUT ** -0.5) * BETA,
        "ab_q_gain": 1.0 + 0.02 * nrm(ks[9], (N_EVEN, HEAD_DIM), f32),
        "ab_k_gain": 1.0 + 0.02 * nrm(ks[10], (N_EVEN, HEAD_DIM), f32),
        "ab_sink": 0.5 * nrm(ks[11], (N_EVEN, B_HEADS), f32),
        "c_w_in": nrm(ks[12], (N_ODD, D_MODEL, 3 * C_WIDTH), f32) * d_sc,
        "c_w_out": nrm(ks[13], (N_ODD, C_WIDTH, D_MODEL), f32) * (C_WIDTH ** -0.5) * BETA,
        "c_rpb": 0.1 * nrm(ks[14], (N_ODD, C_HEADS, 2 * NA_KH - 1, 2 * NA_KW - 1), f32),
        "mem_w_q": nrm(ks[15], (DEPTH, D_MODEL, D_MODEL), f32) * d_sc,
        "mem_w_kv": nrm(ks[16], (DEPTH, D_MODEL, 2 * D_MODEL), f32) * d_sc,
        "mem_w_o": nrm(ks[17], (DEPTH, D_MODEL, D_MODEL), f32) * d_sc * BETA,
    }


def reference(x, mem, ln_g, ln_b, ffn_w_gate, ffn_w_up, ffn_w_down,
              ab_w_in, ab_w_out, ab_q_gain, ab_k_gain, ab_sink,
              c_w_in, c_w_out, c_rpb, mem_w_q, mem_w_kv, mem_w_o):
    S = x.shape[1]
    t = jnp.arange(S)
    row = (t // GRID_W).astype(jnp.float32)
    colp = (t % GRID_W).astype(jnp.float32)
    ang_2d = jnp.concatenate([rope_angles(row, HEAD_DIM // 2), rope_angles(colp, HEAD_DIM // 2)], axis=-1)
    ang_1d = rope_angles(t.astype(jnp.float32), HEAD_DIM)
    for i in range(DEPTH):
        j = i // 2
        y = swiglu(x, ffn_w_gate[i, 0], ffn_w_up[i, 0], ffn_w_down[i, 0])
        x = layer_norm(ALPHA * x + 0.5 * y, ln_g[i, 0], ln_b[i, 0])
        if i % 2 == 0:
            y = mixer_ab(x, ab_w_in[j], ab_w_out[j], ab_q_gain[j], ab_k_gain[j], ab_sink[j], ang_2d, ang_1d)
        else:
            y = mixer_c(x, c_w_in[j], c_w_out[j], c_rpb[j])
        x = layer_norm(ALPHA * x + y, ln_g[i, 1], ln_b[i, 1])
        y = memory_attn(x, mem, mem_w_q[i], mem_w_kv[i], mem_w_o[i])
        x = layer_norm(ALPHA * x + y, ln_g[i, 2], ln_b[i, 2])
        y = swiglu(x, ffn_w_gate[i, 1], ffn_w_up[i, 1], ffn_w_down[i, 1])
        x = layer_norm(ALPHA * x + 0.5 * y, ln_g[i, 3], ln_b[i, 3])
    return x
```

```python
import contextlib
import os
import numpy as np
DBG = os.environ.get('MK_DBG', '')
import concourse.bass as bass
import concourse.mybir as mybir
from concourse.bass_utils import run_bass_kernel_spmd

F32 = mybir.dt.float32
BF16 = mybir.dt.bfloat16
ALU = mybir.AluOpType
AF = mybir.ActivationFunctionType
AX = mybir.AxisListType

D = 1024
S = 4096
NT = 32
DEPTH = 4
DFF = 2816
NF = 22
ALPHA = (2.0 * DEPTH) ** 0.25
LN_EPS = 1e-5
RMS_EPS = 1e-6
NEG = -30000.0

ENGS = ("pe", "act", "dve", "pool", "sp")


class Res:
    __slots__ = ("last_w", "readers")

    def __init__(self):
        self.last_w = None
        self.readers = []


def mkres(n):
    return [Res() for _ in range(n)]


class Op:
    __slots__ = ("eng", "fn", "deps", "signal", "count", "is_dma", "sem")

    def __init__(self, eng, fn, is_dma):
        self.eng = eng
        self.fn = fn
        self.deps = []
        self.signal = False
        self.count = None
        self.is_dma = is_dma
        self.sem = None


class Prog:
    def __init__(self, n_dma_sems=96):
        self.ops = {e: [] for e in ENGS}
        self.n_dma_sems = n_dma_sems
        self.dma_rr = 0
        self.sw_rr = 0
        self.n_hw = 32
        self.dma_last = [None] * n_dma_sems
        self.dma_cnt = [0] * n_dma_sems

    def emit(self, eng, fn, reads=(), writes=(), dma=False):
        op = Op(eng, fn, dma)
        deps = []
        for r in reads:
            if r.last_w is not None:
                deps.append(r.last_w)
        for w in writes:
            if w.last_w is not None:
                deps.append(w.last_w)
            deps.extend(w.readers)
        if dma:
            if eng == "pool":
                k = self.n_hw + self.sw_rr
                self.sw_rr = (self.sw_rr + 1) % (self.n_dma_sems - self.n_hw)
            else:
                k = self.dma_rr
                self.dma_rr = (k + 1) % self.n_hw
            prev = self.dma_last[k]
            if prev is not None:
                deps.append(prev)
            self.dma_last[k] = op
            self.dma_cnt[k] += 16
            op.sem = k
            op.count = self.dma_cnt[k]
            op.signal = True
        self._adddeps(op, deps)
        for r in reads:
            r.readers.append(op)
        for w in writes:
            w.last_w = op
            w.readers = []
        self.ops[eng].append(op)
        return op

    def _adddeps(self, op, deps):
        seen = set()
        for d in deps:
            if d is op or id(d) in seen:
                continue
            seen.add(id(d))
            if (not d.is_dma) and (not op.is_dma) and d.eng == "pe" and op.eng == "pe" and op.fn is not None:
                continue
            d.signal = True
            op.deps.append(d)

    def barrier(self):
        lasts = []
        for e in ENGS:
            for o in reversed(self.ops[e]):
                if not o.is_dma and o.fn is not None:
                    lasts.append(o)
                    break
        lasts += [d for d in self.dma_last if d is not None]
        for e in ENGS:
            op = Op(e, None, False)
            self._adddeps(op, lasts)
            self.ops[e].append(op)

    def build(self, nc, final_ops=()):
        for e in ENGS:
            c = 0
            for op in self.ops[e]:
                if op.is_dma or op.fn is None:
                    continue
                if op.signal:
                    c += 1
                    op.count = c
        with contextlib.ExitStack() as st:
            esem = {e: st.enter_context(nc.semaphore("s_" + e)) for e in ENGS}
            dsem = [st.enter_context(nc.semaphore("d%d" % i)) for i in range(self.n_dma_sems)]
            block = st.enter_context(nc.Block())

            def semof(op):
                return dsem[op.sem] if op.is_dma else esem[op.eng]

            def replay(ename, eng):
                known = {}
                for op in self.ops[ename]:
                    waits = []
                    for d in op.deps:
                        key = ("d", d.sem) if d.is_dma else ("e", d.eng)
                        if known.get(key, 0) >= d.count:
                            continue
                        known[key] = d.count
                        waits.append((semof(d), d.count))
                    fuse = ename == "pe" and op.fn is not None and len(waits) > 0
                    for sm, v in (waits[:-1] if fuse else waits):
                        eng.wait_ge(sm, v)
                    if op.fn is None:
                        continue
                    ins = op.fn(eng)
                    if fuse:
                        ins._wait_ge(*waits[-1])
                    if op.is_dma:
                        ins.then_inc(dsem[op.sem], 16)
                    elif op.signal:
                        ins.then_inc(esem[ename], 1)
                if ename == "sp":
                    for d in final_ops:
                        eng.wait_ge(semof(d), d.count)

            @block.sync
            def _(eng):
                replay("sp", eng)

            @block.tensor
            def _(eng):
                replay("pe", eng)

            @block.scalar
            def _(eng):
                replay("act", eng)

            @block.vector
            def _(eng):
                replay("dve", eng)

            @block.gpsimd
            def _(eng):
                replay("pool", eng)


class Ring:
    def __init__(self, aps):
        self.aps = aps
        self.res = mkres(len(aps))
        self.i = 0

    def next(self):
        k = self.i
        self.i = (k + 1) % len(self.aps)
        return self.aps[k], self.res[k]


class Builder:
    def __init__(self, n_sub=16):
        self.n_sub = n_sub
        self.nc = bass.Bass("TRN2", target_bir_lowering=False)
        self.P = Prog()
        self.fin = []

    def pipeline(self, n, s1, s23, L=2):
        for idx in range(n + L):
            if idx < n:
                s1(idx)
            if idx - L >= 0:
                s23(idx - L)

    def uname(self, n):
        self._uid = getattr(self, "_uid", 0) + 1
        return "%s_%d" % (n, self._uid)

    def mm(self, out, ores, lhsT, lres, rhs, rres, start, stop):
        self.P.emit("pe", lambda e: e.matmul(out, lhsT=lhsT, rhs=rhs, start=start, stop=stop),
                    reads=list(lres) + list(rres), writes=[ores])

    def tr(self, out, ores, in_, ires):
        idb = self.idb
        self.P.emit("pe", lambda e: e.transpose(out=out, in_=in_, identity=idb[:]),
                    reads=list(ires) + [self.r_idb], writes=[ores])

    def act(self, out, ores, in_, ires, func, scale=1.0, bias=None, extra_reads=()):
        if bias is None:
            fn = lambda e: e.activation(out=out, in_=in_, func=func, scale=scale)
        else:
            fn = lambda e: e.activation(out=out, in_=in_, func=func, scale=scale, bias=bias)
        self.P.emit("act", fn, reads=list(ires) + list(extra_reads), writes=list(ores))

    def tt(self, eng, out, ores, in0, in1, ires, op):
        self.P.emit(eng, lambda e: e.tensor_tensor(out=out, in0=in0, in1=in1, op=op),
                    reads=list(ires), writes=list(ores))

    def ts(self, eng, out, ores, in0, ires, s1, s2, op0, op1=None):
        if op1 is None:
            fn = lambda e: e.tensor_scalar(out=out, in0=in0, scalar1=s1, scalar2=None, op0=op0)
        else:
            fn = lambda e: e.tensor_scalar(out=out, in0=in0, scalar1=s1, scalar2=s2, op0=op0, op1=op1)
        self.P.emit(eng, fn, reads=list(ires), writes=list(ores))

    def stt(self, eng, out, ores, in0, scalar, in1, ires, op0, op1):
        self.P.emit(eng, lambda e: e.scalar_tensor_tensor(out=out, in0=in0, scalar=scalar, in1=in1, op0=op0, op1=op1),
                    reads=list(ires), writes=list(ores))

    def cp(self, eng, out, ores, in_, ires):
        if eng == "act":
            self.P.emit("act", lambda e: e.copy(out=out, in_=in_), reads=list(ires), writes=list(ores))
        else:
            self.P.emit(eng, lambda e: e.tensor_copy(out=out, in_=in_), reads=list(ires), writes=list(ores))

    def dma(self, eng, out, ores, in_, ires):
        return self.P.emit(eng, lambda e: e.dma_start(out=out, in_=in_), reads=list(ires), writes=list(ores), dma=True)

    def declare(self):
        nc = self.nc
        di = lambda n, s, dt=F32: nc.dram_tensor(n, s, dt, kind="ExternalInput").ap()
        self.x = di("x", [S, D])
        self.mem = di("mem", [256, D])
        self.ln_g = di("ln_g", [DEPTH, 4, D])
        self.ln_b = di("ln_b", [DEPTH, 4, D])
        self.w_gate = di("ffn_w_gate", [DEPTH, 2, D, DFF])
        self.w_up = di("ffn_w_up", [DEPTH, 2, D, DFF])
        self.w_down = di("ffn_w_down", [DEPTH, 2, DFF, D])
        self.ab_w_in = di("ab_w_in", [2, D, 1536])
        self.ab_w_out = di("ab_w_out", [2, D, D])
        self.ab_q_gain = di("ab_q_gain", [2, 64])
        self.ab_k_gain = di("ab_k_gain", [2, 64])
        self.ab_sink = di("ab_sink", [2, 8])
        self.c_w_in = di("c_w_in", [2, D, 3072])
        self.c_w_out = di("c_w_out", [2, D, D])
        self.mem_w_q = di("mem_w_q", [DEPTH, D, D])
        self.mem_w_kv = di("mem_w_kv", [DEPTH, D, 2 * D])
        self.mem_w_o = di("mem_w_o", [DEPTH, D, D])
        self.c_ident = di("c_ident", [128, 128])
        self.c_rope = di("c_rope", [4, 128, NT * 32])
        self.c_bmask = di("c_bmask", [128, 3 * 128])
        self.c_cmask = di("c_cmask", [128, 64])
        self.c_rpbT = di("c_rpbT", [2, 16, 128, 14 * 64])
        self.y = nc.dram_tensor("y", [S, D], F32, kind="ExternalOutput").ap()
        dn = lambda n, s, dt=BF16: nc.dram_tensor(n, s, dt, kind="Internal").ap()
        self.b_gate = dn("b_gate", [DEPTH, 2, 11, 128, 2048])
        self.b_up = dn("b_up", [DEPTH, 2, 11, 128, 2048])
        self.b_down = dn("b_down", [DEPTH, 2, DFF, D])
        self.b_ab_in = dn("b_ab_in", [2, D, 1536])
        self.b_ab_out = dn("b_ab_out", [2, D, D])
        self.b_c_in = dn("b_c_in", [2, D, 3072])
        self.b_c_out = dn("b_c_out", [2, D, D])
        self.b_mq = dn("b_mq", [DEPTH, D, D])
        self.b_mkv = dn("b_mkv", [DEPTH, D, 2 * D])
        self.b_mo = dn("b_mo", [DEPTH, D, D])
        self.oT_d = dn("oT_d", [8, 128, S])
        self.xres = dn("xres", [S, D], F32)
        self.r_w = {}
        self.r_y = mkres(NT)
        self.r_oT = [mkres(8) for _ in range(8)]

    def plan_conversions(self):
        q = []
        for i in range(DEPTH):
            j = i // 2
            for k in range(2):
                if k == 1:
                    if i % 2 == 0:
                        q.append((("ab_in", j), self.b_ab_in[j], self.ab_w_in[j]))
                        q.append((("ab_out", j), self.b_ab_out[j], self.ab_w_out[j]))
                    else:
                        q.append((("c_in", j), self.b_c_in[j], self.c_w_in[j]))
                        q.append((("c_out", j), self.b_c_out[j], self.c_w_out[j]))
                    q.append((("mkv", i), self.b_mkv[i], self.mem_w_kv[i]))
                    q.append((("mq", i), self.b_mq[i], self.mem_w_q[i]))
                    q.append((("mo", i), self.b_mo[i], self.mem_w_o[i]))
                for nm, bdst, wsrc in (("gate", self.b_gate, self.w_gate), ("up", self.b_up, self.w_up)):
                    wv = wsrc[i, k].rearrange("(kc p) n -> p kc n", p=128)
                    for fg in range(11):
                        q.append(((nm, i, k, fg), bdst[i, k, fg].rearrange("p (kc c) -> p kc c", c=256), wv[:, :, fg * 256:(fg + 1) * 256]))
                q.append((("down", i, k), self.b_down[i, k], self.w_down[i, k]))
        self.convq = q
        self.convi = 0

    def pump(self, n=1):
        while n > 0 and self.convi < len(self.convq):
            key, dst, src = self.convq[self.convi]
            self.convi += 1
            n -= 1
            r = Res()
            self.r_w[key] = r
            self.dma("pool", dst, [r], src, [])

    def ensure(self, key):
        while key not in self.r_w:
            self.pump(1)
        return self.r_w[key]

    def epilogue(self, t, halves, sub):
        k = self.ep_i
        self.ep_i += 1
        xin, r_xin = self.xin[k % 3][:], self.r_xin[k % 3]
        xo, r_xo = self.xo[k % 2][:], self.r_xo[k % 2]
        xb, r_xb = self.xb[k % 4][:], self.r_xb[k % 4]
        st = self.stt_t[k % 2]
        r_st = self.r_stt[k % 2]
        gb = self.gb[sub % 2]
        r_gb = self.r_gb[sub % 2]
        src = self.x if sub == 0 else self.xres
        dst = self.y if sub == self.n_sub - 1 else self.xres
        rows = slice(t * 128, (t + 1) * 128)
        rd = [] if sub == 0 else [self.r_y[t]]
        self.dma("sp", xin, [r_xin], src[rows, :], rd)
        for h, (pap, pres) in enumerate(halves):
            cs = slice(h * 512, (h + 1) * 512)
            self.stt("dve", xo[:, cs], [r_xo], xin[:, cs], ALPHA, pap, [r_xin, pres, r_xo], ALU.mult, ALU.add)
            self.P.emit("dve", (lambda e, o=st[:, h * 6:(h + 1) * 6], i=xo[:, cs]: e.bn_stats(out=o, in_=i)),
                        reads=[r_xo], writes=[r_st])
        self.P.emit("dve", (lambda e, o=st[:, 12:14], i=st[:, 0:12]: e.bn_aggr(out=o, in_=i)), reads=[r_st], writes=[r_st])
        self.ts("dve", st[:, 14:15], [r_st], st[:, 13:14], [r_st], LN_EPS, None, ALU.add)
        self.tt("pool", st[:, 14:15], [r_st], st[:, 14:15], self.mhalf[:, 0:1], [r_st, self.r_mhalf], ALU.pow)
        self.stt("dve", st[:, 15:16], [r_st], st[:, 12:13], -1.0, st[:, 14:15], [r_st], ALU.mult, ALU.mult)
        if True:
            self.ts("dve", xo, [r_xo], xo, [r_xo, r_st], st[:, 14:15], st[:, 15:16], ALU.mult, ALU.add)
            self.tt("dve", xo, [r_xo], xo, gb[:, 0, :], [r_xo, r_gb], ALU.mult)
        else:
            self.act(xo, [r_xo], xo, [r_xo, r_st], AF.Identity, scale=st[:, 14:15], bias=st[:, 15:16])
            self.tt("pool", xo, [r_xo], xo, gb[:, 0, :], [r_xo, r_gb], ALU.mult)
        self.tt("dve", xo, [r_xo], xo, gb[:, 1, :], [r_xo, r_gb], ALU.add)
        self.cp("pool", xb, [r_xb], xo, [r_xo])
        op = self.dma("pool", dst[rows, :], [self.r_y[t]], xo, [r_xo])
        if sub == self.n_sub - 1:
            self.fin.append(op)

        def fin(t=t, xb=xb, r_xb=r_xb):
            pT, r_pT = self.pT, self.r_pT
            for kc in range(8):
                self.tr(pT[:, kc * 128:(kc + 1) * 128], r_pT, xb[:, kc * 128:(kc + 1) * 128], [r_xb])
            self.cp("dve", self.xT[:, :, t * 128:(t + 1) * 128], [self.r_xT[t]],
                    pT[:, :].rearrange("p (k t) -> p k t", t=128), [r_pT])

        self.pend_fin.append(fin)

    def flush_fin(self, keep=0):
        while len(self.pend_fin) > keep:
            self.pend_fin.pop(0)()

    def load_gb(self, sub):
        i, k = sub // 4, sub % 4
        gb, r = self.gb[sub % 2], self.r_gb[sub % 2]
        self.dma("sp", gb[:, 0, :], [r], self.ln_g[i, k:k + 1, :].broadcast_to([128, D]), [])
        self.dma("sp", gb[:, 1, :], [r], self.ln_b[i, k:k + 1, :].broadcast_to([128, D]), [])

    def wview(self, w):
        return w.rearrange("(kc p) n -> p kc n", p=128)

    def ffn(self, i, k, sub):
        nc, P = self.nc, self.P
        self.load_gb(sub)
        self.in_ffn = True
        with contextlib.ExitStack() as st:
            sbt = lambda n, s, dt: st.enter_context(nc.sbuf_tensor(self.uname(n), s, dt))
            wd = sbt("wd", [128, NF, D], BF16)
            wgu = [sbt("wgu%d" % a, [128, 2, 8, 256], BF16) for a in range(3)]
            hT = sbt("hT", [128, NF, 512], BF16)
            sg = [sbt("sg%d" % a, [128, 512], F32) for a in range(2)]
            r_wd = mkres(2)
            r_wgu = mkres(3)
            r_hT = mkres(NF)
            r_sg = mkres(2)
            wdv = self.wview(self.b_down[i, k])
            for h in range(2):
                self.dma("sp", wd[:, h * 11:(h + 1) * 11, :], [r_wd[h]], wdv[:, h * 11:(h + 1) * 11, :], [self.ensure(("down", i, k))])
            cnt = 0
            for ts_ in range(8):
                tok = slice(ts_ * 512, (ts_ + 1) * 512)
                xr = [self.r_xT[tt_] for tt_ in range(ts_ * 4, ts_ * 4 + 4)]
                for fg in range(11):
                    slot = cnt % 3
                    cnt += 1
                    w_ = wgu[slot]
                    self.dma("sp", w_[:, 0, :, :].rearrange("p k c -> p (k c)"), [r_wgu[slot]], self.b_gate[i, k, fg], [self.ensure(("gate", i, k, fg))])
                    self.dma("sp", w_[:, 1, :, :].rearrange("p k c -> p (k c)"), [r_wgu[slot]], self.b_up[i, k, fg], [self.ensure(("up", i, k, fg))])
                    for fl in range(2):
                        f = fg * 2 + fl
                        pg, r_pg = self.ringT.next()
                        for kc in range(8):
                            self.mm(pg, r_pg, w_[:, 0, kc, fl * 128:(fl + 1) * 128], [r_wgu[slot]], self.xT[:, kc, tok], xr, kc == 0, kc == 7)
                        pu, r_pu = self.ringT.next()
                        for kc in range(8):
                            self.mm(pu, r_pu, w_[:, 1, kc, fl * 128:(fl + 1) * 128], [r_wgu[slot]], self.xT[:, kc, tok], xr, kc == 0, kc == 7)
                        s_ = f % 2
                        self.act(sg[s_][:], [r_sg[s_]], pg, [r_pg], AF.Silu)
                        self.stt("dve", hT[:, f, :], [r_hT[f]], sg[s_][:], 0.5, pu, [r_sg[s_], r_pu], ALU.mult, ALU.mult)
                        if f == 0:
                            self.flush_fin(1)
                        if f == 4:
                            self.flush_fin(0)
                for st_ in range(4):
                    halves = []
                    for h in range(2):
                        py, r_py = self.ringA.next()
                        for f in range(NF):
                            self.mm(py, r_py, hT[:, f, st_ * 128:(st_ + 1) * 128], [r_hT[f]],
                                    wd[:, f, h * 512:(h + 1) * 512], [r_wd[f // 11]], f == 0, f == NF - 1)
                        halves.append((py, r_py))
                    self.flush_fin(1)
                    self.epilogue(ts_ * 4 + st_, halves, sub)
            self.flush_fin()
            P.barrier()
        self.in_ffn = False

    def outproj_phase(self, wkey, wsrc, sub):
        nc, P = self.nc, self.P
        with contextlib.ExitStack() as st:
            sbt = lambda n, s, dt: st.enter_context(nc.sbuf_tensor(self.uname(n), s, dt))
            wo = sbt("wo", [128, 8, D], BF16)
            oin = [sbt("oin%d" % a, [128, 8, 512], BF16) for a in range(2)]
            r_wo = Res()
            r_oin = mkres(2)
            self.dma("sp", wo[:], [r_wo], self.wview(wsrc), [self.ensure(wkey)])
            for qt in range(8):
                o_, r_o = oin[qt % 2], r_oin[qt % 2]
                self.dma("sp", o_[:], [r_o], self.oT_d[:, :, qt * 512:(qt + 1) * 512].rearrange("c p t -> p c t"),
                         [self.r_oT[c][qt] for c in range(8)])
                for st_ in range(4):
                    halves = []
                    for h in range(2):
                        py, r_py = self.ringA.next()
                        for kc in range(8):
                            self.mm(py, r_py, o_[:, kc, st_ * 128:(st_ + 1) * 128], [r_o], wo[:, kc, h * 512:(h + 1) * 512], [r_wo], kc == 0, kc == 7)
                        halves.append((py, r_py))
                    self.flush_fin(1)
                    self.epilogue(qt * 4 + st_, halves, sub)
            self.flush_fin()
            P.barrier()

    def memattn(self, i, sub):
        nc, P = self.nc, self.P
        self.load_gb(sub)
        with contextlib.ExitStack() as st0:
            sbt0 = lambda n, s, dt: st0.enter_context(nc.sbuf_tensor(self.uname(n), s, dt))
            kTm = sbt0("kTm", [128, 8, 256], BF16)
            vm = sbt0("vm", [128, 2, D], BF16)
            r_kTm, r_vm = Res(), Res()
            with contextlib.ExitStack() as st:
                sbt = lambda n, s, dt: st.enter_context(nc.sbuf_tensor(self.uname(n), s, dt))
                wkv = sbt("wkv", [128, 8, 2 * D], BF16)
                r_wkv = Res()
                self.dma("sp", wkv[:], [r_wkv], self.wview(self.b_mkv[i]), [self.ensure(("mkv", i))])
                for oc in range(8):
                    pk, r_pk = self.ringT.next()
                    for kc in range(8):
                        self.mm(pk[:, 0:256], r_pk, wkv[:, kc, oc * 128:(oc + 1) * 128], [r_wkv], self.memT[:, kc, :], [self.r_memT], kc == 0, kc == 7)
                    self.cp("act", kTm[:, oc, :], [r_kTm], pk[:, 0:256], [r_pk])
                for mt in range(2):
                    for h in range(2):
                        pv, r_pv = self.ringT.next()
                        for kc in range(8):
                            self.mm(pv, r_pv, self.memT[:, kc, mt * 128:(mt + 1) * 128], [self.r_memT], wkv[:, kc, D + h * 512:D + (h + 1) * 512], [r_wkv], kc == 0, kc == 7)
                        self.cp("dve", vm[:, mt, h * 512:(h + 1) * 512], [r_vm], pv, [r_pv])
                P.barrier()
            with contextlib.ExitStack() as st:
                sbt = lambda n, s, dt: st.enter_context(nc.sbuf_tensor(self.uname(n), s, dt))
                wq = sbt("wq", [128, 8, D], BF16)
                wo = sbt("wo", [128, 8, D], BF16)
                qT = [sbt("qT%d" % a, [128, 8, 512], BF16) for a in range(2)]
                oT = [sbt("oT%d" % a, [128, 8, 512], BF16) for a in range(2)]
                pe_ = [sbt("pe%d" % a, [128, 2, 512], BF16) for a in range(2)]
                rden = [sbt("rden%d" % a, [128, 512], F32) for a in range(2)]
                r_wq, r_wo = Res(), Res()
                r_qT = [mkres(8) for _ in range(2)]
                r_oT = [mkres(8) for _ in range(2)]
                r_pe = [mkres(2) for _ in range(2)]
                r_rden = mkres(2)
                self.dma("sp", wq[:], [r_wq], self.wview(self.b_mq[i]), [self.ensure(("mq", i))])
                self.dma("sp", wo[:], [r_wo], self.wview(self.b_mo[i]), [self.ensure(("mo", i))])
                scale = 256.0 ** -0.5
                hc = 0
                pend_o = []
                for qt in range(8):
                    tok = slice(qt * 512, (qt + 1) * 512)
                    xr = [self.r_xT[t] for t in range(qt * 4, qt * 4 + 4)]
                    q_, rq_ = qT[qt % 2], r_qT[qt % 2]
                    o_, ro_ = oT[qt % 2], r_oT[qt % 2]
                    for oc in range(8):
                        pq, r_pq = self.ringT.next()
                        for kc in range(8):
                            self.mm(pq, r_pq, wq[:, kc, oc * 128:(oc + 1) * 128], [r_wq], self.xT[:, kc, tok], xr, kc == 0, kc == 7)
                        self.cp("act" if oc % 2 else "dve", q_[:, oc, :], [rq_[oc]], pq, [r_pq])
                        if oc == 0:
                            self.flush_fin(1)
                        if oc == 5:
                            self.flush_fin(0)
                    for h in range(4):
                        e_, re_ = pe_[hc % 2], r_pe[hc % 2]
                        rd_, rrd_ = rden[hc % 2], r_rden[hc % 2]
                        hc += 1
                        for mt in range(2):
                            ps, r_ps = self.ringT.next()
                            for c in range(2):
                                self.mm(ps, r_ps, kTm[:, 2 * h + c, mt * 128:(mt + 1) * 128], [r_kTm], q_[:, 2 * h + c, :], [rq_[2 * h + c]], c == 0, c == 1)
                            self.act(e_[:, mt, :], [re_[mt]], ps, [r_ps], AF.Exp, scale=scale)
                        pd, r_pd = self.ringT.next()
                        for mt in range(2):
                            self.mm(pd, r_pd, self.ones[:, 0:128], [self.r_ones], e_[:, mt, :], [re_[mt]], mt == 0, mt == 1)
                        self.act(rd_[:], [rrd_], pd, [r_pd], AF.Ln)
                        self.act(rd_[:], [rrd_], rd_[:], [rrd_], AF.Exp, scale=-1.0)
                        for c in range(2):
                            po, r_po = self.ringT.next()
                            for mt in range(2):
                                self.mm(po, r_po, vm[:, mt, (2 * h + c) * 128:(2 * h + c + 1) * 128], [r_vm], e_[:, mt, :], [re_[mt]], mt == 0, mt == 1)
                            self.tt("dve", o_[:, 2 * h + c, :], [ro_[2 * h + c]], po, rd_[:], [r_po, rrd_], ALU.mult)
                    def outp(qt=qt, o_=o_, ro_=ro_):
                        for st_ in range(4):
                            halves = []
                            for hh in range(2):
                                py, r_py = self.ringA.next()
                                for kc in range(8):
                                    self.mm(py, r_py, o_[:, kc, st_ * 128:(st_ + 1) * 128], [ro_[kc]], wo[:, kc, hh * 512:(hh + 1) * 512], [r_wo], kc == 0, kc == 7)
                                halves.append((py, r_py))
                            self.flush_fin(1)
                            self.epilogue(qt * 4 + st_, halves, sub)

                    pend_o.append(outp)
                    if len(pend_o) > 1:
                        pend_o.pop(0)()
                while pend_o:
                    pend_o.pop(0)()
                self.flush_fin()
                P.barrier()

    def attn_norm(self, po, r_po, hh, stage, r_stage, rec, r_rec, ncols, sink=None, use_act=False):
        O = slice(0, 64) if hh == 0 else slice(64, 128)
        Dn = slice(64, 128) if hh == 0 else slice(0, 64)
        if use_act:
            if sink is not None:
                es, r_es, hidx = sink
                self.ts("dve", rec[Dn, 0:ncols], [r_rec], po[Dn, 0:ncols], [r_po, r_es], es[Dn, hidx:hidx + 1], None, ALU.add)
                self.act(rec[Dn, 0:ncols], [r_rec], rec[Dn, 0:ncols], [r_rec], AF.Ln)
            else:
                self.act(rec[Dn, 0:ncols], [r_rec], po[Dn, 0:ncols], [r_po], AF.Ln)
            self.act(rec[O, 0:ncols], [r_rec], rec[Dn, 0:ncols], [r_rec], AF.Exp, scale=-1.0)
        elif sink is not None:
            es, r_es, hidx = sink
            self.ts("dve", rec[O, 0:ncols], [r_rec], po[Dn, 0:ncols], [r_po, r_es], es[O, hidx:hidx + 1], None, ALU.add)
            self.P.emit("dve", (lambda e, o=rec[O, 0:ncols]: e.reciprocal(out=o, in_=o)), reads=[r_rec], writes=[r_rec])
        else:
            self.P.emit("dve", (lambda e, o=rec[O, 0:ncols], i=po[Dn, 0:ncols]: e.reciprocal(out=o, in_=i)), reads=[r_po], writes=[r_rec])
        self.tt("dve", stage[O, 0:ncols], [r_stage], po[O, 0:ncols], rec[O, 0:ncols], [r_po, r_rec], ALU.mult)

    def mixer_ab(self, i, sub):
        nc, P = self.nc, self.P
        j = i // 2
        self.load_gb(sub)
        with contextlib.ExitStack() as st:
            sbt = lambda n, s, dt: st.enter_context(nc.sbuf_tensor(self.uname(n), s, dt))
            rope = sbt("rope", [128, 4, NT, 32], F32)
            gains = sbt("gains", [128, 320], F32)
            esink = sbt("esink", [128, 8], F32)
            bmask = sbt("bmask", [128, 384], BF16)
            bmask_f = sbt("bmask_f", [128, 384], F32)
            win = [sbt("win%d" % a, [128, 8, 384], BF16) for a in range(2)]
            qT = sbt("qTg", [128, 2, S], BF16)
            kT = sbt("kTg", [128, S], BF16)
            vaug = sbt("vaug", [128, NT, 192], BF16)
            sq = sbt("sq", [128, 320], F32)
            ss = sbt("ss", [128, 16], F32)
            qn = [sbt("qn%d" % a, [128, 320], F32) for a in range(2)]
            tmp = [sbt("rt%d" % a, [128, 160], F32) for a in range(4)]
            qk = [sbt("qk%d" % a, [128, 384], BF16) for a in range(4)]
            pexp = [sbt("pexp%d" % a, [128, 512], BF16) for a in range(3)]
            pmsk = [sbt("pmsk%d" % a, [128, 384], BF16) for a in range(3)]
            pexp2 = [sbt("pexp2_%d" % a, [128, 1024], BF16) for a in range(3)]
            r_pexp2 = mkres(3)
            stage = [sbt("stage%d" % a, [128, 512], BF16) for a in range(2)]
            rec = [sbt("rec%d" % a, [128, 512], F32) for a in range(2)]
            r_rope, r_gains, r_esink, r_bmask = Res(), Res(), Res(), Res()
            r_win = mkres(2)
            r_qT = [mkres(NT) for _ in range(2)]
            r_kT = mkres(NT)
            r_vaug = mkres(NT)
            r_sq, r_ss = Res(), Res()
            r_qn = mkres(2)
            r_tmp = mkres(4)
            r_qk = mkres(4)
            r_pexp = mkres(3)
            r_pmsk = mkres(3)
            r_stage = mkres(2)
            r_rec = mkres(2)
            for a in range(4):
                self.dma("sp", rope[:, a, :, :].rearrange("p t f -> p (t f)"), [r_rope], self.c_rope[a], [])
            for h in range(4):
                self.dma("sp", gains[:, h * 64:(h + 1) * 64], [r_gains], self.ab_q_gain[j:j + 1, :].broadcast_to([128, 64]), [])
            self.dma("sp", gains[:, 256:320], [r_gains], self.ab_k_gain[j:j + 1, :].broadcast_to([128, 64]), [])
            self.dma("sp", esink[:], [r_esink], self.ab_sink[j:j + 1, :].broadcast_to([128, 8]), [])
            self.act(esink[:], [r_esink], esink[:], [r_esink], AF.Exp)
            self.dma("sp", bmask_f[:], [r_bmask], self.c_bmask, [])
            self.cp("dve", bmask[:], [r_bmask], bmask_f[:], [r_bmask])
            self.P.emit("pool", lambda e: e.memset(vaug[:], 1.0), writes=r_vaug)
            wv = self.wview(self.b_ab_in[j])
            scale = 64.0 ** -0.5
            gi = 0
            si = 0
            pi = 0
            for ab in range(2):
                if DBG == 'setup' or ((DBG.startswith('p1') or DBG == 'A') and ab == 1):
                    break
                for g in range(2):
                    if (DBG.startswith('p1') or DBG == 'A') and g == 1:
                        break
                    w_, rw_ = win[gi % 2], r_win[gi % 2]
                    gi += 1
                    base = ab * 768
                    wk = self.ensure(("ab_in", j))
                    self.dma("sp", w_[:, :, 0:256], [rw_], wv[:, :, base + g * 256:base + (g + 1) * 256], [wk])
                    self.dma("sp", w_[:, :, 256:320], [rw_], wv[:, :, base + 512 + g * 64:base + 512 + (g + 1) * 64], [wk])
                    self.dma("sp", w_[:, :, 320:384], [rw_], wv[:, :, base + 640 + g * 64:base + 640 + (g + 1) * 64], [wk])
                    cosv = rope[:, 2 * ab, :, :]
                    sinv = rope[:, 2 * ab + 1, :, :]
                    LIM = int(DBG.split(':')[1]) if DBG.startswith('p1:') else 10**9
                    pend1 = []
                    for t in range(NT):
                        if LIM < 10**9 and t > 0:
                            break
                        pq, r_pq = self.ringT.next()
                        for kc in range(8):
                            if LIM >= 1:
                                self.mm(pq[:, 0:384], r_pq, self.xT[:, kc, t * 128:(t + 1) * 128], [self.r_xT[t]], w_[:, kc, :], [rw_], kc == 0, kc == 7)
                        q_, rq_ = qn[t % 2], r_qn[t % 2]
                        if ab == 0:
                            if LIM >= 2:
                                self.act(sq[:], [r_sq], pq[:, 0:320], [r_pq], AF.Square)
                            if LIM >= 3:
                                self.P.emit("dve", (lambda e, o=ss[:, 0:5], i_=sq[:].rearrange("p (h d) -> p h d", d=64): e.tensor_reduce(out=o, in_=i_, axis=AX.X, op=ALU.add)),
                                            reads=[r_sq], writes=[r_ss])
                            if LIM >= 4:
                                self.ts("dve", ss[:, 8:13], [r_ss], ss[:, 0:5], [r_ss], 1.0 / 64.0, RMS_EPS, ALU.mult, ALU.add)
                            if LIM >= 5:
                                self.tt("pool", ss[:, 8:13], [r_ss], ss[:, 8:13], self.mhalf[:, 0:5], [r_ss, self.r_mhalf], ALU.pow)
                            if LIM >= 6:
                                self.tt("dve", q_[:].rearrange("p (h d) -> p h d", d=64), [rq_], pq[:, 0:320].rearrange("p (h d) -> p h d", d=64),
                                        ss[:, 8:13].unsqueeze(2).broadcast_to([128, 5, 64]), [r_pq, r_ss], ALU.mult)
                            if LIM >= 7:
                                self.tt("pool", q_[:], [rq_], q_[:], gains[:], [rq_, r_gains], ALU.mult)
                        else:
                            if LIM >= 8:
                                self.cp("act", q_[:], [rq_], pq[:, 0:320], [r_pq])
                        k_, rk_ = qk[t % 4], r_qk[t % 4]
                        qv = q_[:].rearrange("p (h d) -> p h d", d=64)
                        x1, x2 = qv[:, :, 0:32], qv[:, :, 32:64]
                        cb = cosv[:, t, :].unsqueeze(1).broadcast_to([128, 5, 32])
                        sb_ = sinv[:, t, :].unsqueeze(1).broadcast_to([128, 5, 32])
                        tv = [tmp[a][:].rearrange("p (h d) -> p h d", d=32) for a in range(4)]
                        kv = k_[:].rearrange("p (h d) -> p h d", d=64)
                        if LIM >= 9:
                            self.tt("dve", tv[0], [r_tmp[0]], x1, cb, [rq_, r_rope], ALU.mult)
                        if LIM >= 10:
                            self.tt("pool", tv[1], [r_tmp[1]], x2, sb_, [rq_, r_rope], ALU.mult)
                        if LIM >= 11:
                            self.tt("dve", kv[:, 0:5, 0:32], [rk_], tv[0], tv[1], [r_tmp[0], r_tmp[1]], ALU.subtract)
                        if LIM >= 12:
                            self.tt("pool", tv[2], [r_tmp[2]], x2, cb, [rq_, r_rope], ALU.mult)
                        if LIM >= 13:
                            self.tt("dve", tv[3], [r_tmp[3]], x1, sb_, [rq_, r_rope], ALU.mult)
                        if LIM >= 14:
                            self.tt("pool", kv[:, 0:5, 32:64], [rk_], tv[2], tv[3], [r_tmp[2], r_tmp[3]], ALU.add)
                        if LIM >= 15:
                            self.cp("act", k_[:, 320:384], [rk_], k_[:, 256:320], [rk_])
                        if LIM >= 16:
                            self.cp("act", vaug[:, t, 64:128], [r_vaug[t]], pq[:, 320:384], [r_pq])
                        def fin1(t=t, k_=k_, rk_=rk_):
                            pT, r_pT = self.pT, self.r_pT
                            for c in range(3):
                                self.tr(pT[:, c * 128:(c + 1) * 128], r_pT, k_[:, c * 128:(c + 1) * 128], [rk_])
                            self.cp("dve", qT[:, :, t * 128:(t + 1) * 128], [r_qT[0][t], r_qT[1][t]],
                                    pT[:, 0:256].rearrange("p (k t) -> p k t", t=128), [r_pT])
                            self.cp("dve", kT[:, t * 128:(t + 1) * 128], [r_kT[t]], pT[:, 256:384], [r_pT])

                        pend1.append(fin1)
                        while len(pend1) > 2:
                            pend1.pop(0)()
                    while pend1:
                        pend1.pop(0)()
                    if not DBG.startswith('p1'):
                        if ab == 0:
                            its = []
                            self.attn_a_pairs(g, qT, r_qT, kT, r_kT, vaug, r_vaug, pexp2, r_pexp2, stage, r_stage, rec, r_rec, scale)
                        else:
                            its = [(jc, qt, hh, nb) for jc in range(2) for qt in range(8) for hh in range(2) for nb in range(4)]
                        state = {}
                        cnt = {"si": si, "pi": pi}

                        def s1(ix, its=its, state=state, cnt=cnt, ab=ab, g=g):
                            jc, qt, hh, kb = its[ix]
                            P0 = slice(hh * 64, (hh + 1) * 64)
                            if hh == 0 and kb == 0:
                                k2 = cnt["si"] % 2
                                cnt["si"] += 1
                                state[("st", jc, qt)] = (stage[k2], r_stage[k2], rec[k2], r_rec[k2])
                            if kb == 0:
                                state[("po", jc, qt, hh)] = self.ringA.next()
                            ps, r_ps = self.ringT.next()
                            state[ix] = (ps, r_ps)
                            if ab == 0:
                                qr = [r_qT[jc][t] for t in range(qt * 4, qt * 4 + 4)]
                                self.mm(ps, r_ps, kT[P0, kb * 128:(kb + 1) * 128], [r_kT[kb]], qT[P0, jc, qt * 512:(qt + 1) * 512], qr, True, True)
                            else:
                                n = qt * 4 + kb
                                kbs = [m for m in (n - 1, n, n + 1) if 0 <= m < NT]
                                for idx, m in enumerate(kbs):
                                    self.mm(ps[:, idx * 128:(idx + 1) * 128], r_ps, kT[P0, m * 128:(m + 1) * 128], [r_kT[m]],
                                            qT[P0, jc, n * 128:(n + 1) * 128], [r_qT[jc][n]], True, True)

                        def s23(ix, its=its, state=state, cnt=cnt, ab=ab, g=g):
                            jc, qt, hh, kb = its[ix]
                            chunk = ab * 4 + g * 2 + jc
                            vcols = slice(64, 192) if hh == 0 else slice(0, 128)
                            ps, r_ps = state.pop(ix)
                            po, r_po = state[("po", jc, qt, hh)]
                            stg, r_stg, rc, r_rc = state[("st", jc, qt)]
                            p3 = cnt["pi"] % 3
                            p2 = cnt["pi"] % 3
                            cnt["pi"] += 1
                            e_, re_ = pexp[p3], r_pexp[p3]
                            if ab == 0:
                                self.act(e_[:], [re_], ps, [r_ps], AF.Exp, scale=scale)
                                self.mm(po, r_po, vaug[:, kb, vcols], [r_vaug[kb]], e_[:], [re_], kb == 0, kb == NT - 1)
                                if kb == NT - 1:
                                    self.attn_norm(po, r_po, hh, stg, r_stg, rc, r_rc, 512)
                                    if hh == 1:
                                        self.dma("sp", self.oT_d[chunk, :, qt * 512:(qt + 1) * 512], [self.r_oT[chunk][qt]], stg[:], [r_stg])
                            else:
                                nb = kb
                                n = qt * 4 + nb
                                kbs = [m for m in (n - 1, n, n + 1) if 0 <= m < NT]
                                m0 = 0 if n > 0 else 1
                                nk = len(kbs)
                                pm, rpm = pmsk[p2], r_pmsk[p2]
                                self.act(e_[:, 0:nk * 128], [re_], ps[:, 0:nk * 128], [r_ps], AF.Exp, scale=scale)
                                self.tt("pool", pm[:, 0:nk * 128], [rpm], e_[:, 0:nk * 128], bmask[:, m0 * 128:(m0 + nk) * 128], [re_, r_bmask], ALU.mult)
                                for idx, m in enumerate(kbs):
                                    self.mm(po[:, nb * 128:(nb + 1) * 128], r_po, vaug[:, m, vcols], [r_vaug[m]], pm[:, idx * 128:(idx + 1) * 128], [rpm],
                                            idx == 0, idx == nk - 1)
                                if nb == 3:
                                    hidx = g * 4 + jc * 2 + hh
                                    self.attn_norm(po, r_po, hh, stg, r_stg, rc, r_rc, 512, sink=(esink, r_esink, hidx), use_act=True)
                                    if hh == 1:
                                        self.dma("sp", self.oT_d[chunk, :, qt * 512:(qt + 1) * 512], [self.r_oT[chunk][qt]], stg[:], [r_stg])

                        if its:
                            self.pipeline(len(its), s1, s23, 2)
                        si, pi = cnt["si"], cnt["pi"]
            P.barrier()
        self.outproj_phase(("ab_out", j), self.b_ab_out[j], sub)

    def attn_a_pairs(self, g, qT, r_qT, kT, r_kT, vaug, r_vaug, pexp2, r_pexp2, stage, r_stage, rec, r_rec, scale):
        its = [(jc, qt, kb) for jc in range(2) for qt in range(8) for kb in range(NT)]
        state = {}
        cnt = {"si": 0, "pi": 0, "ti": self.ringT.i // 2}
        self.ringT.i = ((self.ringT.i + 1) // 2 * 2) % 4
        poring = Ring(self.ringA.aps + [self.pTf])
        poring.res = self.ringA.res + [self.r_pT]

        def s1(ix):
            jc, qt, kb = its[ix]
            if kb == 0:
                k2 = cnt["si"] % 2
                cnt["si"] += 1
                state[("st", jc, qt)] = (stage[k2], r_stage[k2], rec[k2], r_rec[k2])
                state[("po", jc, qt)] = (poring.next(), poring.next())
            i0 = self.ringT.i
            (p0, r0) = self.ringT.next()
            (p1, r1) = self.ringT.next()
            pair = self.pairT[i0 // 2]
            state[ix] = (pair, r0, r1)
            qr = [r_qT[jc][t] for t in range(qt * 4, qt * 4 + 4)]
            self.mm(p0, r0, kT[0:64, kb * 128:(kb + 1) * 128], [r_kT[kb]], qT[0:64, jc, qt * 512:(qt + 1) * 512], qr, True, True)
            self.mm(p1, r1, kT[64:128, kb * 128:(kb + 1) * 128], [r_kT[kb]], qT[64:128, jc, qt * 512:(qt + 1) * 512], qr, True, True)

        def s23(ix):
            jc, qt, kb = its[ix]
            chunk = g * 2 + jc
            pair, r0, r1 = state.pop(ix)
            (po0, rp0), (po1, rp1) = state[("po", jc, qt)]
            stg, r_stg, rc, r_rc = state[("st", jc, qt)]
            p3 = cnt["pi"] % 3
            cnt["pi"] += 1
            e_, re_ = pexp2[p3], r_pexp2[p3]
            self.act(e_[:], [re_], pair, [r0, r1], AF.Exp, scale=scale)
            if ix % 12 == 0:
                self.pump(1)
            self.mm(po0, rp0, vaug[:, kb, 64:192], [r_vaug[kb]], e_[:, 0:512], [re_], kb == 0, kb == NT - 1)
            self.mm(po1, rp1, vaug[:, kb, 0:128], [r_vaug[kb]], e_[:, 512:1024], [re_], kb == 0, kb == NT - 1)
            if kb == NT - 1:
                self.attn_norm(po0, rp0, 0, stg, r_stg, rc, r_rc, 512)
                self.attn_norm(po1, rp1, 1, stg, r_stg, rc, r_rc, 512)
                self.dma("sp", self.oT_d[chunk, :, qt * 512:(qt + 1) * 512], [self.r_oT[chunk][qt]], stg[:], [r_stg])

        self.pipeline(len(its), s1, s23, 1)

    def mixer_c(self, i, sub):
        nc, P = self.nc, self.P
        j = i // 2
        self.load_gb(sub)
        with contextlib.ExitStack() as st:
            sbt = lambda n, s, dt: st.enter_context(nc.sbuf_tensor(self.uname(n), s, dt))
            cmask = sbt("cmask", [128, 64], F32)
            wc = [sbt("wc%d" % a, [128, 8, 3, 128], BF16) for a in range(2)]
            qT = sbt("qTc", [128, S], BF16)
            kT = sbt("kTc", [128, S], BF16)
            vaug = sbt("vaugc", [128, 63, 192], BF16)
            bias = [sbt("bias%d" % a, [128, 14, 64], F32) for a in range(2)]
            s_sb = [sbt("s_sb%d" % a, [128, 256], F32) for a in range(4)]
            pexp = [sbt("pexpc%d" % a, [128, 256], BF16) for a in range(5)]
            stage = [sbt("stagec%d" % a, [128, 512], BF16) for a in range(2)]
            rec = [sbt("recc%d" % a, [128, 512], F32) for a in range(2)]
            r_cmask = Res()
            r_wc = mkres(2)
            r_qT = mkres(8)
            r_kT = mkres(8)
            r_vaug = mkres(63)
            r_bias = mkres(2)
            r_s = mkres(4)
            r_pexp = mkres(5)
            r_stage = mkres(2)
            r_rec = mkres(2)
            self.dma("sp", cmask[:], [r_cmask], self.c_cmask, [])
            self.P.emit("pool", lambda e: e.memset(vaug[:], 1.0), writes=r_vaug)
            wv = self.wview(self.b_c_in[j])
            wk = self.ensure(("c_in", j))
            scale = 64.0 ** -0.5
            si = 0
            pi = 0
            for c in range(8):
                w_, rw_ = wc[c % 2], r_wc[c % 2]
                for a in range(3):
                    self.dma("sp", w_[:, :, a, :], [rw_], wv[:, :, a * D + c * 128:a * D + (c + 1) * 128], [wk])
                for hh in range(2):
                    b_, rb_ = bias[hh], r_bias[hh]
                    self.dma("sp", b_[:].rearrange("p e q -> p (e q)"), [rb_], self.c_rpbT[j, 2 * c + hh], [])
                    self.tt("pool", b_[:], [rb_], b_[:], cmask[:].unsqueeze(1).broadcast_to([128, 14, 64]), [rb_, r_cmask], ALU.add)
                for nt_ in range(8):
                    tok = slice(nt_ * 512, (nt_ + 1) * 512)
                    xr = [self.r_xT[t] for t in range(nt_ * 4, nt_ * 4 + 4)]
                    for a, (dst, rdst) in enumerate(((qT, r_qT), (kT, r_kT))):
                        pq, r_pq = self.ringT.next()
                        for kc in range(8):
                            self.mm(pq, r_pq, w_[:, kc, a, :], [rw_], self.xT[:, kc, tok], xr, kc == 0, kc == 7)
                        self.cp("act" if a == 0 else "dve", dst[:, tok], [rdst[nt_]], pq, [r_pq])
                for v4 in range(16):
                    tiles = [v4 * 4 + a for a in range(4) if v4 * 4 + a < 63]
                    pv, r_pv = self.ringT.next()
                    for a, ti in enumerate(tiles):
                        xr = [self.r_xT[(ti * 64) // 128], self.r_xT[min((ti * 64 + 127) // 128, NT - 1)]]
                        for kc in range(8):
                            self.mm(pv[:, a * 128:(a + 1) * 128], r_pv, self.xT[:, kc, ti * 64:ti * 64 + 128], xr, w_[:, kc, 2, :], [rw_], kc == 0, kc == 7)
                    n_ = len(tiles)
                    vv = vaug[:, v4 * 4:v4 * 4 + n_, :]
                    pvv = pv[:, 0:n_ * 128].rearrange("p (a d) -> p a d", d=128)
                    rv = [r_vaug[ti] for ti in tiles]
                    self.cp("dve", vv[:, :, 0:64], rv, pvv[:, :, 0:64], [r_pv])
                    self.cp("act", vv[:, :, 128:192], rv, pvv[:, :, 64:128], [r_pv])
                its = [(band, hh, rl) for band in range(8) for hh in range(2) for rl in range(8)]
                state = {}
                cnt = {"si": si, "pi": pi}

                def s1(ix, its=its, state=state, cnt=cnt, c=c):
                    band, hh, rl = its[ix]
                    P0 = slice(hh * 64, (hh + 1) * 64)
                    if hh == 0 and rl == 0:
                        k2 = cnt["si"] % 2
                        cnt["si"] += 1
                        state[("st", band)] = (stage[k2], r_stage[k2], rec[k2], r_rec[k2])
                    if rl == 0:
                        state[("po", band, hh)] = self.ringA.next()
                    r = band * 8 + rl
                    rs = min(max(r - 4, 0), 56)
                    ps, r_ps = self.ringT.next()
                    state[ix] = (ps, r_ps)
                    for jb in range(4):
                        k0 = (rs + 2 * jb) * 64
                        self.mm(ps[:, jb * 64:(jb + 1) * 64], r_ps, kT[P0, k0:k0 + 128], [r_kT[k0 // 512], r_kT[(k0 + 127) // 512]],
                                qT[P0, r * 64:(r + 1) * 64], [r_qT[r // 8]], True, True)

                def s23(ix, its=its, state=state, cnt=cnt, c=c):
                    band, hh, rl = its[ix]
                    vcols = slice(0, 128) if hh == 0 else slice(64, 192)
                    b_, rb_ = bias[hh], r_bias[hh]
                    r = band * 8 + rl
                    rs = min(max(r - 4, 0), 56)
                    e0 = rs - r + 7
                    ps, r_ps = state.pop(ix)
                    po, r_po = state[("po", band, hh)]
                    stg, r_stg, rc, r_rc = state[("st", band)]
                    p3, p2 = cnt["pi"] % 5, cnt["pi"] % 4
                    cnt["pi"] += 1
                    s_, rs_ = s_sb[p2], r_s[p2]
                    e_, re_ = pexp[p3], r_pexp[p3]
                    self.stt("dve", s_[:].rearrange("p (a q) -> p a q", q=64), [rs_], ps[:, 0:256].rearrange("p (a q) -> p a q", q=64), scale,
                             b_[:, e0:e0 + 7:2, :], [r_ps, rb_], ALU.mult, ALU.add)
                    self.act(e_[:], [re_], s_[:], [rs_], AF.Exp)
                    if ix % 2 == 0 and c == 0:
                        self.pump(1)
                    for jb in range(4):
                        ti = rs + 2 * jb
                        self.mm(po[:, rl * 64:(rl + 1) * 64], r_po, vaug[:, ti, vcols], [r_vaug[ti]], e_[:, jb * 64:(jb + 1) * 64], [re_], jb == 0, jb == 3)
                    if rl == 7:
                        self.attn_norm(po, r_po, hh, stg, r_stg, rc, r_rc, 512, use_act=True)
                        if hh == 1:
                            self.dma("sp", self.oT_d[c, :, band * 512:(band + 1) * 512], [self.r_oT[c][band]], stg[:], [r_stg])

                self.pipeline(len(its), s1, s23, 3)
                si, pi = cnt["si"], cnt["pi"]
            P.barrier()
        self.outproj_phase(("c_out", j), self.b_c_out[j], sub)

    def build(self):
        nc, P = self.nc, self.P
        self.declare()
        with contextlib.ExitStack() as st:
            sbt = lambda n, s, dt: st.enter_context(nc.sbuf_tensor(self.uname(n), s, dt))
            pst = lambda n, s, dt: st.enter_context(nc.psum_tensor(n, s, dt))
            self.xT = sbt("xT", [128, 8, S], BF16)
            self.r_xT = mkres(NT)
            self.idb = sbt("idb", [128, 128], BF16)
            self.r_idb = Res()
            self.ones = sbt("ones", [128, 128], BF16)
            self.r_ones = Res()
            self.mhalf = sbt("mhalf", [128, 16], F32)
            self.r_mhalf = Res()
            self.memT = sbt("memT", [128, 8, 256], BF16)
            self.r_memT = Res()
            self.xin = [sbt("xin%d" % a, [128, D], F32) for a in range(3)]
            self.r_xin = mkres(3)
            self.xo = [sbt("xo%d" % a, [128, D], F32) for a in range(2)]
            self.r_xo = mkres(2)
            self.xb = [sbt("xb%d" % a, [128, D], BF16) for a in range(4)]
            self.r_xb = mkres(4)
            self.stt_t = [sbt("stt%d" % a, [128, 16], F32) for a in range(2)]
            self.r_stt = mkres(2)
            self.gb = [sbt("gb%d" % a, [128, 2, D], F32) for a in range(2)]
            self.r_gb = mkres(2)
            self.ep_i = 0
            self.in_ffn = False
            self.pend_fin = []
            ringA = [pst("pa%d" % a, [128, 512], F32) for a in range(3)]
            ringT2 = [pst("pt%d" % a, [128, 1024], F32) for a in range(2)]
            self.ringA = Ring([a[:] for a in ringA])
            self.ringT = Ring([ringT2[0][:, 0:512], ringT2[0][:, 512:1024], ringT2[1][:, 0:512], ringT2[1][:, 512:1024]])
            self.pairT = [ringT2[0][:], ringT2[1][:]]
            self.pTf = pst("pT", [128, 512], F32)[:]
            self.pT = self.pTf.bitcast(BF16)
            self.r_pT = Res()

            idf, r_idf = self.xin[0], self.r_xin[0]
            self.dma("sp", idf[:, 0:128], [r_idf], self.c_ident, [])
            self.cp("dve", self.idb[:], [self.r_idb], idf[:, 0:128], [r_idf])
            self.P.emit("pool", lambda e: e.memset(self.ones[:], 1.0), writes=[self.r_ones])
            self.P.emit("pool", lambda e: e.memset(self.mhalf[:], -0.5), writes=[self.r_mhalf])
            self.plan_conversions()
            for t in range(NT + 2):
                k = t % 3
                xin, r_xin = self.xin[k], self.r_xin[k]
                xb, r_xb = self.xb[t % 2], self.r_xb[t % 2]
                src = self.x[t * 128:(t + 1) * 128, :] if t < NT else self.mem[(t - NT) * 128:(t - NT + 1) * 128, :]
                self.dma("sp", xin[:], [r_xin], src, [])
                self.cp("act", xb[:], [r_xb], xin[:], [r_xin])
                for kc in range(8):
                    self.tr(self.pT[:, kc * 128:(kc + 1) * 128], self.r_pT, xb[:, kc * 128:(kc + 1) * 128], [r_xb])
                pv = self.pT[:, :].rearrange("p (k t) -> p k t", t=128)
                if t < NT:
                    self.cp("dve", self.xT[:, :, t * 128:(t + 1) * 128], [self.r_xT[t]], pv, [self.r_pT])
                else:
                    m = t - NT
                    self.cp("dve", self.memT[:, :, m * 128:(m + 1) * 128], [self.r_memT], pv, [self.r_pT])
            sub = 0
            for i in range(DEPTH):
                for k in range(4):
                    if sub >= self.n_sub:
                        break
                    if k == 0:
                        self.ffn(i, 0, sub)
                    elif k == 1:
                        if i % 2 == 0:
                            self.mixer_ab(i, sub)
                        else:
                            self.mixer_c(i, sub)
                    elif k == 2:
                        self.memattn(i, sub)
                    else:
                        self.ffn(i, 1, sub)
                    sub += 1
            P.build(nc, final_ops=self.fin)
        return nc


def _rope_tables():
    t = np.arange(S)
    row = (t // 64).astype(np.float32)
    col = (t % 64).astype(np.float32)

    def ang(pos, dim):
        inv = np.float32(10000.0) ** (-(np.arange(0, dim, 2, dtype=np.float32)) / np.float32(dim))
        return (pos[:, None] * inv[None, :].astype(np.float32)).astype(np.float32)

    a2 = np.concatenate([ang(row, 32), ang(col, 32)], axis=-1)
    a1 = ang(t.astype(np.float32), 64)
    tb = np.stack([np.cos(a2), np.sin(a2), np.cos(a1), np.sin(a1)]).astype(np.float32)
    tb = tb.reshape(4, NT, 128, 32).transpose(0, 2, 1, 3)
    return np.ascontiguousarray(tb.reshape(4, 128, NT * 32))


def _consts():
    ident = np.eye(128, dtype=np.float32)
    b = np.arange(128)[:, None]
    a = np.arange(128)[None, :]
    bmask = np.concatenate([(a <= b), np.ones((128, 128), bool), (b <= a)], axis=1).astype(np.float32)
    kc = (np.arange(128) % 64)[:, None]
    qc = np.arange(64)[None, :]
    cs = np.clip(qc - 8, 0, 48)
    cmask = np.where((kc >= cs) & (kc <= cs + 15), 0.0, NEG).astype(np.float32)
    return ident, bmask, cmask


def _rpb_gather(c_rpb):
    p = np.arange(128)
    krl = (p // 64)[:, None, None]
    kc = (p % 64)[:, None, None]
    e = np.arange(14)[None, :, None]
    qc = np.arange(64)[None, None, :]
    ri = np.broadcast_to(e + krl, (128, 14, 64))
    ci = np.broadcast_to(np.clip(kc - qc + 15, 0, 30), (128, 14, 64))
    g = c_rpb[:, :, ri, ci]
    return np.ascontiguousarray(g.reshape(2, 16, 128, 14 * 64)).astype(np.float32)


_CACHE = {}


def kernel(x, mem, ln_g, ln_b, ffn_w_gate, ffn_w_up, ffn_w_down,
           ab_w_in, ab_w_out, ab_q_gain, ab_k_gain, ab_sink,
           c_w_in, c_w_out, c_rpb, mem_w_q, mem_w_kv, mem_w_o, _n_sub=16, _cores=8):
    f = lambda a: np.ascontiguousarray(np.asarray(a, dtype=np.float32))
    if _n_sub not in _CACHE:
        _CACHE[_n_sub] = Builder(_n_sub).build()
    nc = _CACHE[_n_sub]
    ident, bmask, cmask = _consts()
    shared = {
        "ln_g": f(ln_g), "ln_b": f(ln_b), "ffn_w_gate": f(ffn_w_gate), "ffn_w_up": f(ffn_w_up),
        "ffn_w_down": f(ffn_w_down), "ab_w_in": f(ab_w_in), "ab_w_out": f(ab_w_out),
        "ab_q_gain": f(ab_q_gain), "ab_k_gain": f(ab_k_gain), "ab_sink": f(ab_sink),
        "c_w_in": f(c_w_in), "c_w_out": f(c_w_out), "mem_w_q": f(mem_w_q), "mem_w_kv": f(mem_w_kv),
        "mem_w_o": f(mem_w_o), "c_ident": ident, "c_rope": _rope_tables(), "c_bmask": bmask,
        "c_cmask": cmask, "c_rpbT": _rpb_gather(f(c_rpb)),
    }
    xs, ms = f(x), f(mem)
    in_maps = []
    for b in range(_cores):
        d = dict(shared)
        d["x"] = xs[b]
        d["mem"] = ms[b]
        in_maps.append(d)
    res = run_bass_kernel_spmd(nc, in_maps, core_ids=list(range(_cores)))
    return np.stack([np.asarray(r["y"], dtype=np.float32) for r in res.results], axis=0)
```

```python
import contextlib
import os
import numpy as np
DBG = os.environ.get('MK_DBG', '')
import concourse.bass as bass
import concourse.mybir as mybir
from concourse.bass_utils import run_bass_kernel_spmd

F32 = mybir.dt.float32
BF16 = mybir.dt.bfloat16
ALU = mybir.AluOpType
AF = mybir.ActivationFunctionType
AX = mybir.AxisListType

D = 1024
S = 4096
NT = 32
DEPTH = 4
DFF = 2816
NF = 22
ALPHA = (2.0 * DEPTH) ** 0.25
LN_EPS = 1e-5
RMS_EPS = 1e-6
NEG = -30000.0

ENGS = ("pe", "act", "dve", "pool", "sp")


class Res:
    __slots__ = ("last_w", "readers")

    def __init__(self):
        self.last_w = None
        self.readers = []


def mkres(n):
    return [Res() for _ in range(n)]


class Op:
    __slots__ = ("eng", "fn", "deps", "signal", "count", "is_dma", "sem")

    def __init__(self, eng, fn, is_dma):
        self.eng = eng
        self.fn = fn
        self.deps = []
        self.signal = False
        self.count = None
        self.is_dma = is_dma
        self.sem = None


class Prog:
    def __init__(self, n_dma_sems=96):
        self.ops = {e: [] for e in ENGS}
        self.n_dma_sems = n_dma_sems
        self.dma_rr = 0
        self.sw_rr = 0
        self.n_hw = 32
        self.dma_last = [None] * n_dma_sems
        self.dma_cnt = [0] * n_dma_sems

    def emit(self, eng, fn, reads=(), writes=(), dma=False):
        op = Op(eng, fn, dma)
        deps = []
        for r in reads:
            if r.last_w is not None:
                deps.append(r.last_w)
        for w in writes:
            if w.last_w is not None:
                deps.append(w.last_w)
            deps.extend(w.readers)
        if dma:
            if eng == "pool":
                k = self.n_hw + self.sw_rr
                self.sw_rr = (self.sw_rr + 1) % (self.n_dma_sems - self.n_hw)
            else:
                k = self.dma_rr
                self.dma_rr = (k + 1) % self.n_hw
            prev = self.dma_last[k]
            if prev is not None:
                deps.append(prev)
            self.dma_last[k] = op
            self.dma_cnt[k] += 16
            op.sem = k
            op.count = self.dma_cnt[k]
            op.signal = True
        self._adddeps(op, deps)
        for r in reads:
            r.readers.append(op)
        for w in writes:
            w.last_w = op
            w.readers = []
        self.ops[eng].append(op)
        return op

    def _adddeps(self, op, deps):
        seen = set()
        for d in deps:
            if d is op or id(d) in seen:
                continue
            seen.add(id(d))
            if (not d.is_dma) and (not op.is_dma) and d.eng == "pe" and op.eng == "pe" and op.fn is not None:
                continue
            d.signal = True
            op.deps.append(d)

    def barrier(self):
        lasts = []
        for e in ENGS:
            for o in reversed(self.ops[e]):
                if not o.is_dma and o.fn is not None:
                    lasts.append(o)
                    break
        lasts += [d for d in self.dma_last if d is not None]
        for e in ENGS:
            op = Op(e, None, False)
            self._adddeps(op, lasts)
            self.ops[e].append(op)

    def build(self, nc, final_ops=()):
        for e in ENGS:
            c = 0
            for op in self.ops[e]:
                if op.is_dma or op.fn is None:
                    continue
                if op.signal:
                    c += 1
                    op.count = c
        with contextlib.ExitStack() as st:
            esem = {e: st.enter_context(nc.semaphore("s_" + e)) for e in ENGS}
            dsem = [st.enter_context(nc.semaphore("d%d" % i)) for i in range(self.n_dma_sems)]
            block = st.enter_context(nc.Block())

            def semof(op):
                return dsem[op.sem] if op.is_dma else esem[op.eng]

            def replay(ename, eng):
                known = {}
                for op in self.ops[ename]:
                    waits = []
                    for d in op.deps:
                        key = ("d", d.sem) if d.is_dma else ("e", d.eng)
                        if known.get(key, 0) >= d.count:
                            continue
                        known[key] = d.count
                        waits.append((semof(d), d.count))
                    fuse = ename == "pe" and op.fn is not None and len(waits) > 0
                    for sm, v in (waits[:-1] if fuse else waits):
                        eng.wait_ge(sm, v)
                    if op.fn is None:
                        continue
                    ins = op.fn(eng)
                    if fuse:
                        ins._wait_ge(*waits[-1])
                    if op.is_dma:
                        ins.then_inc(dsem[op.sem], 16)
                    elif op.signal:
                        ins.then_inc(esem[ename], 1)
                if ename == "sp":
                    for d in final_ops:
                        eng.wait_ge(semof(d), d.count)

            @block.sync
            def _(eng):
                replay("sp", eng)

            @block.tensor
            def _(eng):
                replay("pe", eng)

            @block.scalar
            def _(eng):
                replay("act", eng)

            @block.vector
            def _(eng):
                replay("dve", eng)

            @block.gpsimd
            def _(eng):
                replay("pool", eng)


class Ring:
    def __init__(self, aps):
        self.aps = aps
        self.res = mkres(len(aps))
        self.i = 0

    def next(self):
        k = self.i
        self.i = (k + 1) % len(self.aps)
        return self.aps[k], self.res[k]


class Builder:
    def __init__(self, n_sub=16):
        self.n_sub = n_sub
        self.nc = bass.Bass("TRN2", target_bir_lowering=False)
        self.P = Prog()
        self.fin = []

    def pipeline(self, n, s1, s23, L=2):
        for idx in range(n + L):
            if idx < n:
                s1(idx)
            if idx - L >= 0:
                s23(idx - L)

    def uname(self, n):
        self._uid = getattr(self, "_uid", 0) + 1
        return "%s_%d" % (n, self._uid)

    def mm(self, out, ores, lhsT, lres, rhs, rres, start, stop):
        self.P.emit("pe", lambda e: e.matmul(out, lhsT=lhsT, rhs=rhs, start=start, stop=stop),
                    reads=list(lres) + list(rres), writes=[ores])

    def tr(self, out, ores, in_, ires):
        idb = self.idb
        self.P.emit("pe", lambda e: e.transpose(out=out, in_=in_, identity=idb[:]),
                    reads=list(ires) + [self.r_idb], writes=[ores])

    def act(self, out, ores, in_, ires, func, scale=1.0, bias=None, extra_reads=()):
        if bias is None:
            fn = lambda e: e.activation(out=out, in_=in_, func=func, scale=scale)
        else:
            fn = lambda e: e.activation(out=out, in_=in_, func=func, scale=scale, bias=bias)
        self.P.emit("act", fn, reads=list(ires) + list(extra_reads), writes=list(ores))

    def tt(self, eng, out, ores, in0, in1, ires, op):
        self.P.emit(eng, lambda e: e.tensor_tensor(out=out, in0=in0, in1=in1, op=op),
                    reads=list(ires), writes=list(ores))

    def ts(self, eng, out, ores, in0, ires, s1, s2, op0, op1=None):
        if op1 is None:
            fn = lambda e: e.tensor_scalar(out=out, in0=in0, scalar1=s1, scalar2=None, op0=op0)
        else:
            fn = lambda e: e.tensor_scalar(out=out, in0=in0, scalar1=s1, scalar2=s2, op0=op0, op1=op1)
        self.P.emit(eng, fn, reads=list(ires), writes=list(ores))

    def stt(self, eng, out, ores, in0, scalar, in1, ires, op0, op1):
        self.P.emit(eng, lambda e: e.scalar_tensor_tensor(out=out, in0=in0, scalar=scalar, in1=in1, op0=op0, op1=op1),
                    reads=list(ires), writes=list(ores))

    def cp(self, eng, out, ores, in_, ires):
        if eng == "act":
            self.P.emit("act", lambda e: e.copy(out=out, in_=in_), reads=list(ires), writes=list(ores))
        else:
            self.P.emit(eng, lambda e: e.tensor_copy(out=out, in_=in_), reads=list(ires), writes=list(ores))

    def dma(self, eng, out, ores, in_, ires):
        return self.P.emit(eng, lambda e: e.dma_start(out=out, in_=in_), reads=list(ires), writes=list(ores), dma=True)

    def declare(self):
        nc = self.nc
        di = lambda n, s, dt=F32: nc.dram_tensor(n, s, dt, kind="ExternalInput").ap()
        self.x = di("x", [S, D])
        self.mem = di("mem", [256, D])
        self.ln_g = di("ln_g", [DEPTH, 4, D])
        self.ln_b = di("ln_b", [DEPTH, 4, D])
        self.w_gate = di("ffn_w_gate", [DEPTH, 2, D, DFF])
        self.w_up = di("ffn_w_up", [DEPTH, 2, D, DFF])
        self.w_down = di("ffn_w_down", [DEPTH, 2, DFF, D])
        self.ab_w_in = di("ab_w_in", [2, D, 1536])
        self.ab_w_out = di("ab_w_out", [2, D, D])
        self.ab_q_gain = di("ab_q_gain", [2, 64])
        self.ab_k_gain = di("ab_k_gain", [2, 64])
        self.ab_sink = di("ab_sink", [2, 8])
        self.c_w_in = di("c_w_in", [2, D, 3072])
        self.c_w_out = di("c_w_out", [2, D, D])
        self.mem_w_q = di("mem_w_q", [DEPTH, D, D])
        self.mem_w_kv = di("mem_w_kv", [DEPTH, D, 2 * D])
        self.mem_w_o = di("mem_w_o", [DEPTH, D, D])
        self.c_ident = di("c_ident", [128, 128])
        self.c_rope = di("c_rope", [4, 128, NT * 32])
        self.c_bmask = di("c_bmask", [128, 3 * 128])
        self.c_cmask = di("c_cmask", [128, 64])
        self.c_rpbT = di("c_rpbT", [2, 16, 128, 14 * 64])
        self.y = nc.dram_tensor("y", [S, D], F32, kind="ExternalOutput").ap()
        dn = lambda n, s, dt=BF16: nc.dram_tensor(n, s, dt, kind="Internal").ap()
        self.b_gate = dn("b_gate", [DEPTH, 2, 11, 128, 2048])
        self.b_up = dn("b_up", [DEPTH, 2, 11, 128, 2048])
        self.b_down = dn("b_down", [DEPTH, 2, DFF, D])
        self.b_ab_in = dn("b_ab_in", [2, D, 1536])
        self.b_ab_out = dn("b_ab_out", [2, D, D])
        self.b_c_in = dn("b_c_in", [2, D, 3072])
        self.b_c_out = dn("b_c_out", [2, D, D])
        self.b_mq = dn("b_mq", [DEPTH, D, D])
        self.b_mkv = dn("b_mkv", [DEPTH, D, 2 * D])
        self.b_mo = dn("b_mo", [DEPTH, D, D])
        self.oT_d = dn("oT_d", [8, 128, S])
        self.xres = dn("xres", [S, D], F32)
        self.r_w = {}
        self.r_y = mkres(NT)
        self.r_oT = [mkres(8) for _ in range(8)]

    def plan_conversions(self):
        q = []
        for i in range(DEPTH):
            j = i // 2
            for k in range(2):
                if k == 1:
                    if i % 2 == 0:
                        q.append((("ab_in", j), self.b_ab_in[j], self.ab_w_in[j]))
                        q.append((("ab_out", j), self.b_ab_out[j], self.ab_w_out[j]))
                    else:
                        q.append((("c_in", j), self.b_c_in[j], self.c_w_in[j]))
                        q.append((("c_out", j), self.b_c_out[j], self.c_w_out[j]))
                    q.append((("mkv", i), self.b_mkv[i], self.mem_w_kv[i]))
                    q.append((("mq", i), self.b_mq[i], self.mem_w_q[i]))
                    q.append((("mo", i), self.b_mo[i], self.mem_w_o[i]))
                for nm, bdst, wsrc in (("gate", self.b_gate, self.w_gate), ("up", self.b_up, self.w_up)):
                    wv = wsrc[i, k].rearrange("(kc p) n -> p kc n", p=128)
                    for fg in range(11):
                        q.append(((nm, i, k, fg), bdst[i, k, fg].rearrange("p (kc c) -> p kc c", c=256), wv[:, :, fg * 256:(fg + 1) * 256]))
                q.append((("down", i, k), self.b_down[i, k], self.w_down[i, k]))
        self.convq = q
        self.convi = 0

    def pump(self, n=1):
        while n > 0 and self.convi < len(self.convq):
            key, dst, src = self.convq[self.convi]
            self.convi += 1
            n -= 1
            r = Res()
            self.r_w[key] = r
            self.dma("pool", dst, [r], src, [])

    def ensure(self, key):
        while key not in self.r_w:
            self.pump(1)
        return self.r_w[key]

    def epilogue(self, t, halves, sub):
        k = self.ep_i
        self.ep_i += 1
        xin, r_xin = self.xin[k % 3][:], self.r_xin[k % 3]
        xo, r_xo = self.xo[k % 2][:], self.r_xo[k % 2]
        xb, r_xb = self.xb[k % 4][:], self.r_xb[k % 4]
        st = self.stt_t[k % 2]
        r_st = self.r_stt[k % 2]
        gb = self.gb[sub % 2]
        r_gb = self.r_gb[sub % 2]
        src = self.x if sub == 0 else self.xres
        dst = self.y if sub == self.n_sub - 1 else self.xres
        rows = slice(t * 128, (t + 1) * 128)
        rd = [] if sub == 0 else [self.r_y[t]]
        self.dma("sp", xin, [r_xin], src[rows, :], rd)
        for h, (pap, pres) in enumerate(halves):
            cs = slice(h * 512, (h + 1) * 512)
            self.stt("dve", xo[:, cs], [r_xo], xin[:, cs], ALPHA, pap, [r_xin, pres, r_xo], ALU.mult, ALU.add)
            self.P.emit("dve", (lambda e, o=st[:, h * 6:(h + 1) * 6], i=xo[:, cs]: e.bn_stats(out=o, in_=i)),
                        reads=[r_xo], writes=[r_st])
        self.P.emit("dve", (lambda e, o=st[:, 12:14], i=st[:, 0:12]: e.bn_aggr(out=o, in_=i)), reads=[r_st], writes=[r_st])
        self.ts("dve", st[:, 14:15], [r_st], st[:, 13:14], [r_st], LN_EPS, None, ALU.add)
        self.tt("pool", st[:, 14:15], [r_st], st[:, 14:15], self.mhalf[:, 0:1], [r_st, self.r_mhalf], ALU.pow)
        self.stt("dve", st[:, 15:16], [r_st], st[:, 12:13], -1.0, st[:, 14:15], [r_st], ALU.mult, ALU.mult)
        if True:
            self.ts("dve", xo, [r_xo], xo, [r_xo, r_st], st[:, 14:15], st[:, 15:16], ALU.mult, ALU.add)
            self.tt("dve", xo, [r_xo], xo, gb[:, 0, :], [r_xo, r_gb], ALU.mult)
        else:
            self.act(xo, [r_xo], xo, [r_xo, r_st], AF.Identity, scale=st[:, 14:15], bias=st[:, 15:16])
            self.tt("pool", xo, [r_xo], xo, gb[:, 0, :], [r_xo, r_gb], ALU.mult)
        self.tt("dve", xo, [r_xo], xo, gb[:, 1, :], [r_xo, r_gb], ALU.add)
        self.cp("pool", xb, [r_xb], xo, [r_xo])
        op = self.dma("pool", dst[rows, :], [self.r_y[t]], xo, [r_xo])
        if sub == self.n_sub - 1:
            self.fin.append(op)

        def fin(t=t, xb=xb, r_xb=r_xb):
            pT, r_pT = self.pT, self.r_pT
            for kc in range(8):
                self.tr(pT[:, kc * 128:(kc + 1) * 128], r_pT, xb[:, kc * 128:(kc + 1) * 128], [r_xb])
            self.cp("dve", self.xT[:, :, t * 128:(t + 1) * 128], [self.r_xT[t]],
                    pT[:, :].rearrange("p (k t) -> p k t", t=128), [r_pT])

        self.pend_fin.append(fin)

    def flush_fin(self, keep=0):
        while len(self.pend_fin) > keep:
            self.pend_fin.pop(0)()

    def load_gb(self, sub):
        i, k = sub // 4, sub % 4
        gb, r = self.gb[sub % 2], self.r_gb[sub % 2]
        self.dma("sp", gb[:, 0, :], [r], self.ln_g[i, k:k + 1, :].broadcast_to([128, D]), [])
        self.dma("sp", gb[:, 1, :], [r], self.ln_b[i, k:k + 1, :].broadcast_to([128, D]), [])

    def wview(self, w):
        return w.rearrange("(kc p) n -> p kc n", p=128)

    def ffn(self, i, k, sub):
        nc, P = self.nc, self.P
        self.load_gb(sub)
        self.in_ffn = True
        with contextlib.ExitStack() as st:
            sbt = lambda n, s, dt: st.enter_context(nc.sbuf_tensor(self.uname(n), s, dt))
            wd = sbt("wd", [128, NF, D], BF16)
            wgu = [sbt("wgu%d" % a, [128, 2, 8, 256], BF16) for a in range(3)]
            hT = sbt("hT", [128, NF, 512], BF16)
            sg = [sbt("sg%d" % a, [128, 512], F32) for a in range(2)]
            r_wd = mkres(2)
            r_wgu = mkres(3)
            r_hT = mkres(NF)
            r_sg = mkres(2)
            wdv = self.wview(self.b_down[i, k])
            for h in range(2):
                self.dma("sp", wd[:, h * 11:(h + 1) * 11, :], [r_wd[h]], wdv[:, h * 11:(h + 1) * 11, :], [self.ensure(("down", i, k))])
            cnt = 0
            for ts_ in range(8):
                tok = slice(ts_ * 512, (ts_ + 1) * 512)
                xr = [self.r_xT[tt_] for tt_ in range(ts_ * 4, ts_ * 4 + 4)]
                for fg in range(11):
                    slot = cnt % 3
                    cnt += 1
                    w_ = wgu[slot]
                    self.dma("sp", w_[:, 0, :, :].rearrange("p k c -> p (k c)"), [r_wgu[slot]], self.b_gate[i, k, fg], [self.ensure(("gate", i, k, fg))])
                    self.dma("sp", w_[:, 1, :, :].rearrange("p k c -> p (k c)"), [r_wgu[slot]], self.b_up[i, k, fg], [self.ensure(("up", i, k, fg))])
                    for fl in range(2):
                        f = fg * 2 + fl
                        pg, r_pg = self.ringT.next()
                        for kc in range(8):
                            self.mm(pg, r_pg, w_[:, 0, kc, fl * 128:(fl + 1) * 128], [r_wgu[slot]], self.xT[:, kc, tok], xr, kc == 0, kc == 7)
                        pu, r_pu = self.ringT.next()
                        for kc in range(8):
                            self.mm(pu, r_pu, w_[:, 1, kc, fl * 128:(fl + 1) * 128], [r_wgu[slot]], self.xT[:, kc, tok], xr, kc == 0, kc == 7)
                        s_ = f % 2
                        self.act(sg[s_][:], [r_sg[s_]], pg, [r_pg], AF.Silu)
                        self.stt("dve", hT[:, f, :], [r_hT[f]], sg[s_][:], 0.5, pu, [r_sg[s_], r_pu], ALU.mult, ALU.mult)
                        if f == 0:
                            self.flush_fin(1)
                        if f == 4:
                            self.flush_fin(0)
                for st_ in range(4):
                    halves = []
                    for h in range(2):
                        py, r_py = self.ringA.next()
                        for f in range(NF):
                            self.mm(py, r_py, hT[:, f, st_ * 128:(st_ + 1) * 128], [r_hT[f]],
                                    wd[:, f, h * 512:(h + 1) * 512], [r_wd[f // 11]], f == 0, f == NF - 1)
                        halves.append((py, r_py))
                    self.flush_fin(1)
                    self.epilogue(ts_ * 4 + st_, halves, sub)
            self.flush_fin()
            P.barrier()
        self.in_ffn = False

    def outproj_phase(self, wkey, wsrc, sub):
        nc, P = self.nc, self.P
        with contextlib.ExitStack() as st:
            sbt = lambda n, s, dt: st.enter_context(nc.sbuf_tensor(self.uname(n), s, dt))
            wo = sbt("wo", [128, 8, D], BF16)
            oin = [sbt("oin%d" % a, [128, 8, 512], BF16) for a in range(2)]
            r_wo = Res()
            r_oin = mkres(2)
            self.dma("sp", wo[:], [r_wo], self.wview(wsrc), [self.ensure(wkey)])
            for qt in range(8):
                o_, r_o = oin[qt % 2], r_oin[qt % 2]
                self.dma("sp", o_[:], [r_o], self.oT_d[:, :, qt * 512:(qt + 1) * 512].rearrange("c p t -> p c t"),
                         [self.r_oT[c][qt] for c in range(8)])
                for st_ in range(4):
                    halves = []
                    for h in range(2):
                        py, r_py = self.ringA.next()
                        for kc in range(8):
                            self.mm(py, r_py, o_[:, kc, st_ * 128:(st_ + 1) * 128], [r_o], wo[:, kc, h * 512:(h + 1) * 512], [r_wo], kc == 0, kc == 7)
                        halves.append((py, r_py))
                    self.flush_fin(1)
                    self.epilogue(qt * 4 + st_, halves, sub)
            self.flush_fin()
            P.barrier()

    def memattn(self, i, sub):
        nc, P = self.nc, self.P
        self.load_gb(sub)
        with contextlib.ExitStack() as st0:
            sbt0 = lambda n, s, dt: st0.enter_context(nc.sbuf_tensor(self.uname(n), s, dt))
            kTm = sbt0("kTm", [128, 8, 256], BF16)
            vm = sbt0("vm", [128, 2, D], BF16)
            r_kTm, r_vm = Res(), Res()
            with contextlib.ExitStack() as st:
                sbt = lambda n, s, dt: st.enter_context(nc.sbuf_tensor(self.uname(n), s, dt))
                wkv = sbt("wkv", [128, 8, 2 * D], BF16)
                r_wkv = Res()
                self.dma("sp", wkv[:], [r_wkv], self.wview(self.b_mkv[i]), [self.ensure(("mkv", i))])
                for oc in range(8):
                    pk, r_pk = self.ringT.next()
                    for kc in range(8):
                        self.mm(pk[:, 0:256], r_pk, wkv[:, kc, oc * 128:(oc + 1) * 128], [r_wkv], self.memT[:, kc, :], [self.r_memT], kc == 0, kc == 7)
                    self.cp("act", kTm[:, oc, :], [r_kTm], pk[:, 0:256], [r_pk])
                for mt in range(2):
                    for h in range(2):
                        pv, r_pv = self.ringT.next()
                        for kc in range(8):
                            self.mm(pv, r_pv, self.memT[:, kc, mt * 128:(mt + 1) * 128], [self.r_memT], wkv[:, kc, D + h * 512:D + (h + 1) * 512], [r_wkv], kc == 0, kc == 7)
                        self.cp("dve", vm[:, mt, h * 512:(h + 1) * 512], [r_vm], pv, [r_pv])
                P.barrier()
            with contextlib.ExitStack() as st:
                sbt = lambda n, s, dt: st.enter_context(nc.sbuf_tensor(self.uname(n), s, dt))
                wq = sbt("wq", [128, 8, D], BF16)
                wo = sbt("wo", [128, 8, D], BF16)
                qT = [sbt("qT%d" % a, [128, 8, 512], BF16) for a in range(2)]
                oT = [sbt("oT%d" % a, [128, 8, 512], BF16) for a in range(2)]
                pe_ = [sbt("pe%d" % a, [128, 2, 512], BF16) for a in range(2)]
                rden = [sbt("rden%d" % a, [128, 512], F32) for a in range(2)]
                r_wq, r_wo = Res(), Res()
                r_qT = [mkres(8) for _ in range(2)]
                r_oT = [mkres(8) for _ in range(2)]
                r_pe = [mkres(2) for _ in range(2)]
                r_rden = mkres(2)
                self.dma("sp", wq[:], [r_wq], self.wview(self.b_mq[i]), [self.ensure(("mq", i))])
                self.dma("sp", wo[:], [r_wo], self.wview(self.b_mo[i]), [self.ensure(("mo", i))])
                scale = 256.0 ** -0.5
                hc = 0
                pend_o = []
                for qt in range(8):
                    tok = slice(qt * 512, (qt + 1) * 512)
                    xr = [self.r_xT[t] for t in range(qt * 4, qt * 4 + 4)]
                    q_, rq_ = qT[qt % 2], r_qT[qt % 2]
                    o_, ro_ = oT[qt % 2], r_oT[qt % 2]
                    for oc in range(8):
                        pq, r_pq = self.ringT.next()
                        for kc in range(8):
                            self.mm(pq, r_pq, wq[:, kc, oc * 128:(oc + 1) * 128], [r_wq], self.xT[:, kc, tok], xr, kc == 0, kc == 7)
                        self.cp("act" if oc % 2 else "dve", q_[:, oc, :], [rq_[oc]], pq, [r_pq])
                        if oc == 0:
                            self.flush_fin(1)
                        if oc == 5:
                            self.flush_fin(0)
                    for h in range(4):
                        e_, re_ = pe_[hc % 2], r_pe[hc % 2]
                        rd_, rrd_ = rden[hc % 2], r_rden[hc % 2]
                        hc += 1
                        for mt in range(2):
                            ps, r_ps = self.ringT.next()
                            for c in range(2):
                                self.mm(ps, r_ps, kTm[:, 2 * h + c, mt * 128:(mt + 1) * 128], [r_kTm], q_[:, 2 * h + c, :], [rq_[2 * h + c]], c == 0, c == 1)
                            self.act(e_[:, mt, :], [re_[mt]], ps, [r_ps], AF.Exp, scale=scale)
                        pd, r_pd = self.ringT.next()
                        for mt in range(2):
                            self.mm(pd, r_pd, self.ones[:, 0:128], [self.r_ones], e_[:, mt, :], [re_[mt]], mt == 0, mt == 1)
                        self.act(rd_[:], [rrd_], pd, [r_pd], AF.Ln)
                        self.act(rd_[:], [rrd_], rd_[:], [rrd_], AF.Exp, scale=-1.0)
                        for c in range(2):
                            po, r_po = self.ringT.next()
                            for mt in range(2):
                                self.mm(po, r_po, vm[:, mt, (2 * h + c) * 128:(2 * h + c + 1) * 128], [r_vm], e_[:, mt, :], [re_[mt]], mt == 0, mt == 1)
                            self.tt("dve", o_[:, 2 * h + c, :], [ro_[2 * h + c]], po, rd_[:], [r_po, rrd_], ALU.mult)
                    def outp(qt=qt, o_=o_, ro_=ro_):
                        for st_ in range(4):
                            halves = []
                            for hh in range(2):
                                py, r_py = self.ringA.next()
                                for kc in range(8):
                                    self.mm(py, r_py, o_[:, kc, st_ * 128:(st_ + 1) * 128], [ro_[kc]], wo[:, kc, hh * 512:(hh + 1) * 512], [r_wo], kc == 0, kc == 7)
                                halves.append((py, r_py))
                            self.flush_fin(1)
                            self.epilogue(qt * 4 + st_, halves, sub)

                    pend_o.append(outp)
                    if len(pend_o) > 1:
                        pend_o.pop(0)()
                while pend_o:
                    pend_o.pop(0)()
                self.flush_fin()
                P.barrier()

    def attn_norm(self, po, r_po, hh, stage, r_stage, rec, r_rec, ncols, sink=None, use_act=False):
        O = slice(0, 64) if hh == 0 else slice(64, 128)
        Dn = slice(64, 128) if hh == 0 else slice(0, 64)
        if use_act:
            if sink is not None:
                es, r_es, hidx = sink
                self.ts("dve", rec[Dn, 0:ncols], [r_rec], po[Dn, 0:ncols], [r_po, r_es], es[Dn, hidx:hidx + 1], None, ALU.add)
                self.act(rec[Dn, 0:ncols], [r_rec], rec[Dn, 0:ncols], [r_rec], AF.Ln)
            else:
                self.act(rec[Dn, 0:ncols], [r_rec], po[Dn, 0:ncols], [r_po], AF.Ln)
            self.act(rec[O, 0:ncols], [r_rec], rec[Dn, 0:ncols], [r_rec], AF.Exp, scale=-1.0)
        elif sink is not None:
            es, r_es, hidx = sink
            self.ts("dve", rec[O, 0:ncols], [r_rec], po[Dn, 0:ncols], [r_po, r_es], es[O, hidx:hidx + 1], None, ALU.add)
            self.P.emit("dve", (lambda e, o=rec[O, 0:ncols]: e.reciprocal(out=o, in_=o)), reads=[r_rec], writes=[r_rec])
        else:
            self.P.emit("dve", (lambda e, o=rec[O, 0:ncols], i=po[Dn, 0:ncols]: e.reciprocal(out=o, in_=i)), reads=[r_po], writes=[r_rec])
        self.tt("dve", stage[O, 0:ncols], [r_stage], po[O, 0:ncols], rec[O, 0:ncols], [r_po, r_rec], ALU.mult)

    def mixer_ab(self, i, sub):
        nc, P = self.nc, self.P
        j = i // 2
        self.load_gb(sub)
        with contextlib.ExitStack() as st:
            sbt = lambda n, s, dt: st.enter_context(nc.sbuf_tensor(self.uname(n), s, dt))
            rope = sbt("rope", [128, 4, NT, 32], F32)
            gains = sbt("gains", [128, 320], F32)
            esink = sbt("esink", [128, 8], F32)
            bmask = sbt("bmask", [128, 384], BF16)
            bmask_f = sbt("bmask_f", [128, 384], F32)
            win = [sbt("win%d" % a, [128, 8, 384], BF16) for a in range(2)]
            qT = sbt("qTg", [128, 2, S], BF16)
            kT = sbt("kTg", [128, S], BF16)
            vaug = sbt("vaug", [128, NT, 192], BF16)
            sq = sbt("sq", [128, 320], F32)
            ss = sbt("ss", [128, 16], F32)
            qn = [sbt("qn%d" % a, [128, 320], F32) for a in range(2)]
            tmp = [sbt("rt%d" % a, [128, 160], F32) for a in range(4)]
            qk = [sbt("qk%d" % a, [128, 384], BF16) for a in range(4)]
            pexp = [sbt("pexp%d" % a, [128, 512], BF16) for a in range(3)]
            pmsk = [sbt("pmsk%d" % a, [128, 384], BF16) for a in range(3)]
            pexp2 = [sbt("pexp2_%d" % a, [128, 1024], BF16) for a in range(3)]
            r_pexp2 = mkres(3)
            stage = [sbt("stage%d" % a, [128, 512], BF16) for a in range(2)]
            rec = [sbt("rec%d" % a, [128, 512], F32) for a in range(2)]
            r_rope, r_gains, r_esink, r_bmask = Res(), Res(), Res(), Res()
            r_win = mkres(2)
            r_qT = [mkres(NT) for _ in range(2)]
            r_kT = mkres(NT)
            r_vaug = mkres(NT)
            r_sq, r_ss = Res(), Res()
            r_qn = mkres(2)
            r_tmp = mkres(4)
            r_qk = mkres(4)
            r_pexp = mkres(3)
            r_pmsk = mkres(3)
            r_stage = mkres(2)
            r_rec = mkres(2)
            for a in range(4):
                self.dma("sp", rope[:, a, :, :].rearrange("p t f -> p (t f)"), [r_rope], self.c_rope[a], [])
            for h in range(4):
                self.dma("sp", gains[:, h * 64:(h + 1) * 64], [r_gains], self.ab_q_gain[j:j + 1, :].broadcast_to([128, 64]), [])
            self.dma("sp", gains[:, 256:320], [r_gains], self.ab_k_gain[j:j + 1, :].broadcast_to([128, 64]), [])
            self.dma("sp", esink[:], [r_esink], self.ab_sink[j:j + 1, :].broadcast_to([128, 8]), [])
            self.act(esink[:], [r_esink], esink[:], [r_esink], AF.Exp)
            self.dma("sp", bmask_f[:], [r_bmask], self.c_bmask, [])
            self.cp("dve", bmask[:], [r_bmask], bmask_f[:], [r_bmask])
            self.P.emit("pool", lambda e: e.memset(vaug[:], 1.0), writes=r_vaug)
            wv = self.wview(self.b_ab_in[j])
            scale = 64.0 ** -0.5
            gi = 0
            si = 0
            pi = 0
            for ab in range(2):
                if DBG == 'setup' or ((DBG.startswith('p1') or DBG == 'A') and ab == 1):
                    break
                for g in range(2):
                    if (DBG.startswith('p1') or DBG == 'A') and g == 1:
                        break
                    w_, rw_ = win[gi % 2], r_win[gi % 2]
                    gi += 1
                    base = ab * 768
                    wk = self.ensure(("ab_in", j))
                    self.dma("sp", w_[:, :, 0:256], [rw_], wv[:, :, base + g * 256:base + (g + 1) * 256], [wk])
                    self.dma("sp", w_[:, :, 256:320], [rw_], wv[:, :, base + 512 + g * 64:base + 512 + (g + 1) * 64], [wk])
                    self.dma("sp", w_[:, :, 320:384], [rw_], wv[:, :, base + 640 + g * 64:base + 640 + (g + 1) * 64], [wk])
                    cosv = rope[:, 2 * ab, :, :]
                    sinv = rope[:, 2 * ab + 1, :, :]
                    LIM = int(DBG.split(':')[1]) if DBG.startswith('p1:') else 10**9
                    pend1 = []
                    for t in range(NT):
                        if LIM < 10**9 and t > 0:
                            break
                        pq, r_pq = self.ringT.next()
                        for kc in range(8):
                            if LIM >= 1:
                                self.mm(pq[:, 0:384], r_pq, self.xT[:, kc, t * 128:(t + 1) * 128], [self.r_xT[t]], w_[:, kc, :], [rw_], kc == 0, kc == 7)
                        q_, rq_ = qn[t % 2], r_qn[t % 2]
                        if ab == 0:
                            if LIM >= 2:
                                self.act(sq[:], [r_sq], pq[:, 0:320], [r_pq], AF.Square)
                            if LIM >= 3:
                                self.P.emit("dve", (lambda e, o=ss[:, 0:5], i_=sq[:].rearrange("p (h d) -> p h d", d=64): e.tensor_reduce(out=o, in_=i_, axis=AX.X, op=ALU.add)),
                                            reads=[r_sq], writes=[r_ss])
                            if LIM >= 4:
                                self.ts("dve", ss[:, 8:13], [r_ss], ss[:, 0:5], [r_ss], 1.0 / 64.0, RMS_EPS, ALU.mult, ALU.add)
                            if LIM >= 5:
                                self.tt("pool", ss[:, 8:13], [r_ss], ss[:, 8:13], self.mhalf[:, 0:5], [r_ss, self.r_mhalf], ALU.pow)
                            if LIM >= 6:
                                self.tt("dve", q_[:].rearrange("p (h d) -> p h d", d=64), [rq_], pq[:, 0:320].rearrange("p (h d) -> p h d", d=64),
                                        ss[:, 8:13].unsqueeze(2).broadcast_to([128, 5, 64]), [r_pq, r_ss], ALU.mult)
                            if LIM >= 7:
                                self.tt("pool", q_[:], [rq_], q_[:], gains[:], [rq_, r_gains], ALU.mult)
                        else:
                            if LIM >= 8:
                                self.cp("act", q_[:], [rq_], pq[:, 0:320], [r_pq])
                        k_, rk_ = qk[t % 4], r_qk[t % 4]
                        qv = q_[:].rearrange("p (h d) -> p h d", d=64)
                        x1, x2 = qv[:, :, 0:32], qv[:, :, 32:64]
                        cb = cosv[:, t, :].unsqueeze(1).broadcast_to([128, 5, 32])
                        sb_ = sinv[:, t, :].unsqueeze(1).broadcast_to([128, 5, 32])
                        tv = [tmp[a][:].rearrange("p (h d) -> p h d", d=32) for a in range(4)]
                        kv = k_[:].rearrange("p (h d) -> p h d", d=64)
                        if LIM >= 9:
                            self.tt("dve", tv[0], [r_tmp[0]], x1, cb, [rq_, r_rope], ALU.mult)
                        if LIM >= 10:
                            self.tt("pool", tv[1], [r_tmp[1]], x2, sb_, [rq_, r_rope], ALU.mult)
                        if LIM >= 11:
                            self.tt("dve", kv[:, 0:5, 0:32], [rk_], tv[0], tv[1], [r_tmp[0], r_tmp[1]], ALU.subtract)
                        if LIM >= 12:
                            self.tt("pool", tv[2], [r_tmp[2]], x2, cb, [rq_, r_rope], ALU.mult)
                        if LIM >= 13:
                            self.tt("dve", tv[3], [r_tmp[3]], x1, sb_, [rq_, r_rope], ALU.mult)
                        if LIM >= 14:
                            self.tt("pool", kv[:, 0:5, 32:64], [rk_], tv[2], tv[3], [r_tmp[2], r_tmp[3]], ALU.add)
                        if LIM >= 15:
                            self.cp("act", k_[:, 320:384], [rk_], k_[:, 256:320], [rk_])
                        if LIM >= 16:
                            self.cp("act", vaug[:, t, 64:128], [r_vaug[t]], pq[:, 320:384], [r_pq])
                        def fin1(t=t, k_=k_, rk_=rk_):
                            pT, r_pT = self.pT, self.r_pT
                            for c in range(3):
                                self.tr(pT[:, c * 128:(c + 1) * 128], r_pT, k_[:, c * 128:(c + 1) * 128], [rk_])
                            self.cp("dve", qT[:, :, t * 128:(t + 1) * 128], [r_qT[0][t], r_qT[1][t]],
                                    pT[:, 0:256].rearrange("p (k t) -> p k t", t=128), [r_pT])
                            self.cp("dve", kT[:, t * 128:(t + 1) * 128], [r_kT[t]], pT[:, 256:384], [r_pT])

                        pend1.append(fin1)
                        while len(pend1) > 2:
                            pend1.pop(0)()
                    while pend1:
                        pend1.pop(0)()
                    if not DBG.startswith('p1'):
                        if ab == 0:
                            its = []
                            self.attn_a_pairs(g, qT, r_qT, kT, r_kT, vaug, r_vaug, pexp2, r_pexp2, stage, r_stage, rec, r_rec, scale)
                        else:
                            its = [(jc, qt, hh, nb) for jc in range(2) for qt in range(8) for hh in range(2) for nb in range(4)]
                        state = {}
                        cnt = {"si": si, "pi": pi}

                        def s1(ix, its=its, state=state, cnt=cnt, ab=ab, g=g):
                            jc, qt, hh, kb = its[ix]
                            P0 = slice(hh * 64, (hh + 1) * 64)
                            if hh == 0 and kb == 0:
                                k2 = cnt["si"] % 2
                                cnt["si"] += 1
                                state[("st", jc, qt)] = (stage[k2], r_stage[k2], rec[k2], r_rec[k2])
                            if kb == 0:
                                state[("po", jc, qt, hh)] = self.ringA.next()
                            ps, r_ps = self.ringT.next()
                            state[ix] = (ps, r_ps)
                            if ab == 0:
                                qr = [r_qT[jc][t] for t in range(qt * 4, qt * 4 + 4)]
                                self.mm(ps, r_ps, kT[P0, kb * 128:(kb + 1) * 128], [r_kT[kb]], qT[P0, jc, qt * 512:(qt + 1) * 512], qr, True, True)
                            else:
                                n = qt * 4 + kb
                                kbs = [m for m in (n - 1, n, n + 1) if 0 <= m < NT]
                                for idx, m in enumerate(kbs):
                                    self.mm(ps[:, idx * 128:(idx + 1) * 128], r_ps, kT[P0, m * 128:(m + 1) * 128], [r_kT[m]],
                                            qT[P0, jc, n * 128:(n + 1) * 128], [r_qT[jc][n]], True, True)

                        def s23(ix, its=its, state=state, cnt=cnt, ab=ab, g=g):
                            jc, qt, hh, kb = its[ix]
                            chunk = ab * 4 + g * 2 + jc
                            vcols = slice(64, 192) if hh == 0 else slice(0, 128)
                            ps, r_ps = state.pop(ix)
                            po, r_po = state[("po", jc, qt, hh)]
                            stg, r_stg, rc, r_rc = state[("st", jc, qt)]
                            p3 = cnt["pi"] % 6
                            cnt["pi"] += 1
                            e_, re_ = (pexp[p3], r_pexp[p3]) if p3 < 3 else (pexp2[p3 - 3], r_pexp2[p3 - 3])
                            if ab == 0:
                                self.act(e_[:], [re_], ps, [r_ps], AF.Exp, scale=scale)
                                self.mm(po, r_po, vaug[:, kb, vcols], [r_vaug[kb]], e_[:], [re_], kb == 0, kb == NT - 1)
                                if kb == NT - 1:
                                    self.attn_norm(po, r_po, hh, stg, r_stg, rc, r_rc, 512)
                                    if hh == 1:
                                        self.dma("sp", self.oT_d[chunk, :, qt * 512:(qt + 1) * 512], [self.r_oT[chunk][qt]], stg[:], [r_stg])
                            else:
                                nb = kb
                                n = qt * 4 + nb
                                kbs = [m for m in (n - 1, n, n + 1) if 0 <= m < NT]
                                m0 = 0 if n > 0 else 1
                                nk = len(kbs)
                                pm, rpm = e_, re_
                                self.act(e_[:, 0:nk * 128], [re_], ps[:, 0:nk * 128], [r_ps], AF.Exp, scale=scale)
                                self.tt("pool", e_[:, 0:nk * 128], [re_], e_[:, 0:nk * 128], bmask[:, m0 * 128:(m0 + nk) * 128], [re_, r_bmask], ALU.mult)
                                for idx, m in enumerate(kbs):
                                    self.mm(po[:, nb * 128:(nb + 1) * 128], r_po, vaug[:, m, vcols], [r_vaug[m]], pm[:, idx * 128:(idx + 1) * 128], [rpm],
                                            idx == 0, idx == nk - 1)
                                if nb == 3:
                                    hidx = g * 4 + jc * 2 + hh
                                    self.attn_norm(po, r_po, hh, stg, r_stg, rc, r_rc, 512, sink=(esink, r_esink, hidx), use_act=True)
                                    if hh == 1:
                                        self.dma("sp", self.oT_d[chunk, :, qt * 512:(qt + 1) * 512], [self.r_oT[chunk][qt]], stg[:], [r_stg])

                        if its:
                            self.pipeline(len(its), s1, s23, 3)
                        si, pi = cnt["si"], cnt["pi"]
            P.barrier()
        self.outproj_phase(("ab_out", j), self.b_ab_out[j], sub)

    def attn_a_pairs(self, g, qT, r_qT, kT, r_kT, vaug, r_vaug, pexp2, r_pexp2, stage, r_stage, rec, r_rec, scale):
        its = [(jc, qt, kb) for jc in range(2) for qt in range(8) for kb in range(NT)]
        state = {}
        cnt = {"si": 0, "pi": 0, "ti": self.ringT.i // 2}
        self.ringT.i = ((self.ringT.i + 1) // 2 * 2) % 4
        poring = Ring(self.ringA.aps + [self.pTf])
        poring.res = self.ringA.res + [self.r_pT]

        def s1(ix):
            jc, qt, kb = its[ix]
            if kb == 0:
                k2 = cnt["si"] % 2
                cnt["si"] += 1
                state[("st", jc, qt)] = (stage[k2], r_stage[k2], rec[k2], r_rec[k2])
                state[("po", jc, qt)] = (poring.next(), poring.next())
            i0 = self.ringT.i
            (p0, r0) = self.ringT.next()
            (p1, r1) = self.ringT.next()
            pair = self.pairT[i0 // 2]
            state[ix] = (pair, r0, r1)
            qr = [r_qT[jc][t] for t in range(qt * 4, qt * 4 + 4)]
            self.mm(p0, r0, kT[0:64, kb * 128:(kb + 1) * 128], [r_kT[kb]], qT[0:64, jc, qt * 512:(qt + 1) * 512], qr, True, True)
            self.mm(p1, r1, kT[64:128, kb * 128:(kb + 1) * 128], [r_kT[kb]], qT[64:128, jc, qt * 512:(qt + 1) * 512], qr, True, True)

        def s23(ix):
            jc, qt, kb = its[ix]
            chunk = g * 2 + jc
            pair, r0, r1 = state.pop(ix)
            (po0, rp0), (po1, rp1) = state[("po", jc, qt)]
            stg, r_stg, rc, r_rc = state[("st", jc, qt)]
            p3 = cnt["pi"] % 3
            cnt["pi"] += 1
            e_, re_ = pexp2[p3], r_pexp2[p3]
            self.act(e_[:], [re_], pair, [r0, r1], AF.Exp, scale=scale)
            if ix % 12 == 0:
                self.pump(1)
            self.mm(po0, rp0, vaug[:, kb, 64:192], [r_vaug[kb]], e_[:, 0:512], [re_], kb == 0, kb == NT - 1)
            self.mm(po1, rp1, vaug[:, kb, 0:128], [r_vaug[kb]], e_[:, 512:1024], [re_], kb == 0, kb == NT - 1)
            if kb == NT - 1:
                self.attn_norm(po0, rp0, 0, stg, r_stg, rc, r_rc, 512)
                self.attn_norm(po1, rp1, 1, stg, r_stg, rc, r_rc, 512)
                self.dma("sp", self.oT_d[chunk, :, qt * 512:(qt + 1) * 512], [self.r_oT[chunk][qt]], stg[:], [r_stg])

        self.pipeline(len(its), s1, s23, 1)

    def mixer_c(self, i, sub):
        nc, P = self.nc, self.P
        j = i // 2
        self.load_gb(sub)
        with contextlib.ExitStack() as st:
            sbt = lambda n, s, dt: st.enter_context(nc.sbuf_tensor(self.uname(n), s, dt))
            cmask = sbt("cmask", [128, 64], F32)
            wc = [sbt("wc%d" % a, [128, 8, 3, 128], BF16) for a in range(2)]
            qT = sbt("qTc", [128, S], BF16)
            kT = sbt("kTc", [128, S], BF16)
            vaug = sbt("vaugc", [128, 63, 192], BF16)
            bias = [sbt("bias%d" % a, [128, 14, 64], F32) for a in range(2)]
            s_sb = [sbt("s_sb%d" % a, [128, 256], F32) for a in range(4)]
            pexp = [sbt("pexpc%d" % a, [128, 256], BF16) for a in range(5)]
            stage = [sbt("stagec%d" % a, [128, 512], BF16) for a in range(2)]
            rec = [sbt("recc%d" % a, [128, 512], F32) for a in range(2)]
            r_cmask = Res()
            r_wc = mkres(2)
            r_qT = mkres(8)
            r_kT = mkres(8)
            r_vaug = mkres(63)
            r_bias = mkres(2)
            r_s = mkres(4)
            r_pexp = mkres(5)
            r_stage = mkres(2)
            r_rec = mkres(2)
            self.dma("sp", cmask[:], [r_cmask], self.c_cmask, [])
            self.P.emit("pool", lambda e: e.memset(vaug[:], 1.0), writes=r_vaug)
            wv = self.wview(self.b_c_in[j])
            wk = self.ensure(("c_in", j))
            scale = 64.0 ** -0.5
            si = 0
            pi = 0
            for c in range(8):
                w_, rw_ = wc[c % 2], r_wc[c % 2]
                for a in range(3):
                    self.dma("sp", w_[:, :, a, :], [rw_], wv[:, :, a * D + c * 128:a * D + (c + 1) * 128], [wk])
                for hh in range(2):
                    b_, rb_ = bias[hh], r_bias[hh]
                    self.dma("sp", b_[:].rearrange("p e q -> p (e q)"), [rb_], self.c_rpbT[j, 2 * c + hh], [])
                    self.tt("pool", b_[:], [rb_], b_[:], cmask[:].unsqueeze(1).broadcast_to([128, 14, 64]), [rb_, r_cmask], ALU.add)
                for nt_ in range(8):
                    tok = slice(nt_ * 512, (nt_ + 1) * 512)
                    xr = [self.r_xT[t] for t in range(nt_ * 4, nt_ * 4 + 4)]
                    for a, (dst, rdst) in enumerate(((qT, r_qT), (kT, r_kT))):
                        pq, r_pq = self.ringT.next()
                        for kc in range(8):
                            self.mm(pq, r_pq, w_[:, kc, a, :], [rw_], self.xT[:, kc, tok], xr, kc == 0, kc == 7)
                        self.cp("act" if a == 0 else "dve", dst[:, tok], [rdst[nt_]], pq, [r_pq])
                for v4 in range(16):
                    tiles = [v4 * 4 + a for a in range(4) if v4 * 4 + a < 63]
                    pv, r_pv = self.ringT.next()
                    for a, ti in enumerate(tiles):
                        xr = [self.r_xT[(ti * 64) // 128], self.r_xT[min((ti * 64 + 127) // 128, NT - 1)]]
                        for kc in range(8):
                            self.mm(pv[:, a * 128:(a + 1) * 128], r_pv, self.xT[:, kc, ti * 64:ti * 64 + 128], xr, w_[:, kc, 2, :], [rw_], kc == 0, kc == 7)
                    n_ = len(tiles)
                    vv = vaug[:, v4 * 4:v4 * 4 + n_, :]
                    pvv = pv[:, 0:n_ * 128].rearrange("p (a d) -> p a d", d=128)
                    rv = [r_vaug[ti] for ti in tiles]
                    self.cp("dve", vv[:, :, 0:64], rv, pvv[:, :, 0:64], [r_pv])
                    self.cp("act", vv[:, :, 128:192], rv, pvv[:, :, 64:128], [r_pv])
                its = [(band, hh, rl) for band in range(8) for hh in range(2) for rl in range(8)]
                state = {}
                cnt = {"si": si, "pi": pi}

                def s1(ix, its=its, state=state, cnt=cnt, c=c):
                    band, hh, rl = its[ix]
                    P0 = slice(hh * 64, (hh + 1) * 64)
                    if hh == 0 and rl == 0:
                        k2 = cnt["si"] % 2
                        cnt["si"] += 1
                        state[("st", band)] = (stage[k2], r_stage[k2], rec[k2], r_rec[k2])
                    if rl == 0:
                        state[("po", band, hh)] = self.ringA.next()
                    r = band * 8 + rl
                    rs = min(max(r - 4, 0), 56)
                    ps, r_ps = self.ringT.next()
                    state[ix] = (ps, r_ps)
                    for jb in range(4):
                        k0 = (rs + 2 * jb) * 64
                        self.mm(ps[:, jb * 64:(jb + 1) * 64], r_ps, kT[P0, k0:k0 + 128], [r_kT[k0 // 512], r_kT[(k0 + 127) // 512]],
                                qT[P0, r * 64:(r + 1) * 64], [r_qT[r // 8]], True, True)

                def s23(ix, its=its, state=state, cnt=cnt, c=c):
                    band, hh, rl = its[ix]
                    vcols = slice(0, 128) if hh == 0 else slice(64, 192)
                    b_, rb_ = bias[hh], r_bias[hh]
                    r = band * 8 + rl
                    rs = min(max(r - 4, 0), 56)
                    e0 = rs - r + 7
                    ps, r_ps = state.pop(ix)
                    po, r_po = state[("po", band, hh)]
                    stg, r_stg, rc, r_rc = state[("st", band)]
                    p3, p2 = cnt["pi"] % 5, cnt["pi"] % 4
                    cnt["pi"] += 1
                    s_, rs_ = s_sb[p2], r_s[p2]
                    e_, re_ = pexp[p3], r_pexp[p3]
                    self.stt("dve", s_[:].rearrange("p (a q) -> p a q", q=64), [rs_], ps[:, 0:256].rearrange("p (a q) -> p a q", q=64), scale,
                             b_[:, e0:e0 + 7:2, :], [r_ps, rb_], ALU.mult, ALU.add)
                    self.act(e_[:], [re_], s_[:], [rs_], AF.Exp)
                    if ix % 2 == 0 and c == 0:
                        self.pump(1)
                    for jb in range(4):
                        ti = rs + 2 * jb
                        self.mm(po[:, rl * 64:(rl + 1) * 64], r_po, vaug[:, ti, vcols], [r_vaug[ti]], e_[:, jb * 64:(jb + 1) * 64], [re_], jb == 0, jb == 3)
                    if rl == 7:
                        self.attn_norm(po, r_po, hh, stg, r_stg, rc, r_rc, 512, use_act=True)
                        if hh == 1:
                            self.dma("sp", self.oT_d[c, :, band * 512:(band + 1) * 512], [self.r_oT[c][band]], stg[:], [r_stg])

                self.pipeline(len(its), s1, s23, 3)
                si, pi = cnt["si"], cnt["pi"]
            P.barrier()
        self.outproj_phase(("c_out", j), self.b_c_out[j], sub)

    def build(self):
        nc, P = self.nc, self.P
        self.declare()
        with contextlib.ExitStack() as st:
            sbt = lambda n, s, dt: st.enter_context(nc.sbuf_tensor(self.uname(n), s, dt))
            pst = lambda n, s, dt: st.enter_context(nc.psum_tensor(n, s, dt))
            self.xT = sbt("xT", [128, 8, S], BF16)
            self.r_xT = mkres(NT)
            self.idb = sbt("idb", [128, 128], BF16)
            self.r_idb = Res()
            self.ones = sbt("ones", [128, 128], BF16)
            self.r_ones = Res()
            self.mhalf = sbt("mhalf", [128, 16], F32)
            self.r_mhalf = Res()
            self.memT = sbt("memT", [128, 8, 256], BF16)
            self.r_memT = Res()
            self.xin = [sbt("xin%d" % a, [128, D], F32) for a in range(3)]
            self.r_xin = mkres(3)
            self.xo = [sbt("xo%d" % a, [128, D], F32) for a in range(2)]
            self.r_xo = mkres(2)
            self.xb = [sbt("xb%d" % a, [128, D], BF16) for a in range(4)]
            self.r_xb = mkres(4)
            self.stt_t = [sbt("stt%d" % a, [128, 16], F32) for a in range(2)]
            self.r_stt = mkres(2)
            self.gb = [sbt("gb%d" % a, [128, 2, D], F32) for a in range(2)]
            self.r_gb = mkres(2)
            self.ep_i = 0
            self.in_ffn = False
            self.pend_fin = []
            ringA = [pst("pa%d" % a, [128, 512], F32) for a in range(3)]
            ringT2 = [pst("pt%d" % a, [128, 1024], F32) for a in range(2)]
            self.ringA = Ring([a[:] for a in ringA])
            self.ringT = Ring([ringT2[0][:, 0:512], ringT2[0][:, 512:1024], ringT2[1][:, 0:512], ringT2[1][:, 512:1024]])
            self.pairT = [ringT2[0][:], ringT2[1][:]]
            self.pTf = pst("pT", [128, 512], F32)[:]
            self.pT = self.pTf.bitcast(BF16)
            self.r_pT = Res()

            idf, r_idf = self.xin[0], self.r_xin[0]
            self.dma("sp", idf[:, 0:128], [r_idf], self.c_ident, [])
            self.cp("dve", self.idb[:], [self.r_idb], idf[:, 0:128], [r_idf])
            self.P.emit("pool", lambda e: e.memset(self.ones[:], 1.0), writes=[self.r_ones])
            self.P.emit("pool", lambda e: e.memset(self.mhalf[:], -0.5), writes=[self.r_mhalf])
            self.plan_conversions()
            for t in range(NT + 2):
                k = t % 3
                xin, r_xin = self.xin[k], self.r_xin[k]
                xb, r_xb = self.xb[t % 2], self.r_xb[t % 2]
                src = self.x[t * 128:(t + 1) * 128, :] if t < NT else self.mem[(t - NT) * 128:(t - NT + 1) * 128, :]
                self.dma("sp", xin[:], [r_xin], src, [])
                self.cp("act", xb[:], [r_xb], xin[:], [r_xin])
                for kc in range(8):
                    self.tr(self.pT[:, kc * 128:(kc + 1) * 128], self.r_pT, xb[:, kc * 128:(kc + 1) * 128], [r_xb])
                pv = self.pT[:, :].rearrange("p (k t) -> p k t", t=128)
                if t < NT:
                    self.cp("dve", self.xT[:, :, t * 128:(t + 1) * 128], [self.r_xT[t]], pv, [self.r_pT])
                else:
                    m = t - NT
                    self.cp("dve", self.memT[:, :, m * 128:(m + 1) * 128], [self.r_memT], pv, [self.r_pT])
            sub = 0
            for i in range(DEPTH):
                for k in range(4):
                    if sub >= self.n_sub:
                        break
                    if k == 0:
                        self.ffn(i, 0, sub)
                    elif k == 1:
                        if i % 2 == 0:
                            self.mixer_ab(i, sub)
                        else:
                            self.mixer_c(i, sub)
                    elif k == 2:
                        self.memattn(i, sub)
                    else:
                        self.ffn(i, 1, sub)
                    sub += 1
            P.build(nc, final_ops=self.fin)
        return nc


def _rope_tables():
    t = np.arange(S)
    row = (t // 64).astype(np.float32)
    col = (t % 64).astype(np.float32)

    def ang(pos, dim):
        inv = np.float32(10000.0) ** (-(np.arange(0, dim, 2, dtype=np.float32)) / np.float32(dim))
        return (pos[:, None] * inv[None, :].astype(np.float32)).astype(np.float32)

    a2 = np.concatenate([ang(row, 32), ang(col, 32)], axis=-1)
    a1 = ang(t.astype(np.float32), 64)
    tb = np.stack([np.cos(a2), np.sin(a2), np.cos(a1), np.sin(a1)]).astype(np.float32)
    tb = tb.reshape(4, NT, 128, 32).transpose(0, 2, 1, 3)
    return np.ascontiguousarray(tb.reshape(4, 128, NT * 32))


def _consts():
    ident = np.eye(128, dtype=np.float32)
    b = np.arange(128)[:, None]
    a = np.arange(128)[None, :]
    bmask = np.concatenate([(a <= b), np.ones((128, 128), bool), (b <= a)], axis=1).astype(np.float32)
    kc = (np.arange(128) % 64)[:, None]
    qc = np.arange(64)[None, :]
    cs = np.clip(qc - 8, 0, 48)
    cmask = np.where((kc >= cs) & (kc <= cs + 15), 0.0, NEG).astype(np.float32)
    return ident, bmask, cmask


def _rpb_gather(c_rpb):
    p = np.arange(128)
    krl = (p // 64)[:, None, None]
    kc = (p % 64)[:, None, None]
    e = np.arange(14)[None, :, None]
    qc = np.arange(64)[None, None, :]
    ri = np.broadcast_to(e + krl, (128, 14, 64))
    ci = np.broadcast_to(np.clip(kc - qc + 15, 0, 30), (128, 14, 64))
    g = c_rpb[:, :, ri, ci]
    return np.ascontiguousarray(g.reshape(2, 16, 128, 14 * 64)).astype(np.float32)


_CACHE = {}


def kernel(x, mem, ln_g, ln_b, ffn_w_gate, ffn_w_up, ffn_w_down,
           ab_w_in, ab_w_out, ab_q_gain, ab_k_gain, ab_sink,
           c_w_in, c_w_out, c_rpb, mem_w_q, mem_w_kv, mem_w_o, _n_sub=16, _cores=8):
    f = lambda a: np.ascontiguousarray(np.asarray(a, dtype=np.float32))
    if _n_sub not in _CACHE:
        _CACHE[_n_sub] = Builder(_n_sub).build()
    nc = _CACHE[_n_sub]
    ident, bmask, cmask = _consts()
    shared = {
        "ln_g": f(ln_g), "ln_b": f(ln_b), "ffn_w_gate": f(ffn_w_gate), "ffn_w_up": f(ffn_w_up),
        "ffn_w_down": f(ffn_w_down), "ab_w_in": f(ab_w_in), "ab_w_out": f(ab_w_out),
        "ab_q_gain": f(ab_q_gain), "ab_k_gain": f(ab_k_gain), "ab_sink": f(ab_sink),
        "c_w_in": f(c_w_in), "c_w_out": f(c_w_out), "mem_w_q": f(mem_w_q), "mem_w_kv": f(mem_w_kv),
        "mem_w_o": f(mem_w_o), "c_ident": ident, "c_rope": _rope_tables(), "c_bmask": bmask,
        "c_cmask": cmask, "c_rpbT": _rpb_gather(f(c_rpb)),
    }
    xs, ms = f(x), f(mem)
    in_maps = []
    for b in range(_cores):
        d = dict(shared)
        d["x"] = xs[b]
        d["mem"] = ms[b]
        in_maps.append(d)
    res = run_bass_kernel_spmd(nc, in_maps, core_ids=list(range(_cores)))
    return np.stack([np.asarray(r["y"], dtype=np.float32) for r in res.results], axis=0)
```
